# Optimizing a Trainium2 kernel written in Bass

```python
import jax, jax.numpy as jnp
from jax import lax
import numpy as np

D_MODEL = 1024
BATCH = 8
SEQ = 4096
DEPTH = 2
DEC_BATCH = 16
DEC_SEQ = 64
PAST_LEN = 1024

CHUNK = 64
POOL_WINDOWS = (2, 4, 8, 16)
POOL_GROUPS = 4
POOL_GROUP_DIM = 128
POOL_WIDTH = POOL_GROUPS * POOL_GROUP_DIM
POOL_OUT_GROUP = D_MODEL // POOL_GROUPS
POOL_HIST = max(POOL_WINDOWS) - 1
CONV_WIDTH = D_MODEL // 2
CONV_K = 3
N_BRANCH = 2
IN_WIDTH = POOL_WIDTH + 3 * CONV_WIDTH + N_BRANCH * D_MODEL
D_FF = 2816
EPS = 1e-6

kernel_name = "hybrid_pool_shortconv_convffn_stream_step"


def rmsnorm(x, g):
    xf = x.astype(jnp.float32)
    y = xf * lax.rsqrt(jnp.mean(xf * xf, axis=-1, keepdims=True) + EPS)
    return (y * g.astype(jnp.float32)).astype(x.dtype)


def causal_dwconv(u, buf, w):
    S = u.shape[1]
    ext = jnp.concatenate([buf.astype(u.dtype), u], axis=1)
    y = sum(ext[:, k:k + S] * w[k] for k in range(CONV_K))
    return y, ext[:, -(CONV_K - 1):]


def pool_mixer(u, buf, pos, w_map, scale):
    B, S, _ = u.shape
    ext = jnp.concatenate([buf.astype(u.dtype), u], axis=1)
    cs = jnp.cumsum(ext.astype(jnp.float32), axis=1)
    cs = jnp.pad(cs, ((0, 0), (1, 0), (0, 0)))
    outs = []
    for gi, win in enumerate(POOL_WINDOWS):
        sl = slice(gi * POOL_GROUP_DIM, (gi + 1) * POOL_GROUP_DIM)
        hi = cs[:, POOL_HIST + 1:POOL_HIST + 1 + S, sl]
        lo = cs[:, POOL_HIST + 1 - win:POOL_HIST + 1 - win + S, sl]
        cnt = jnp.minimum(win, pos + 1).astype(jnp.float32)[None, :, None]
        outs.append((hi - lo) / cnt)
    pooled = jnp.stack(outs, axis=2)
    mixed = (pooled - u.reshape(B, S, POOL_GROUPS, POOL_GROUP_DIM).astype(jnp.float32)).astype(u.dtype)
    y = jnp.einsum('bsgc,gcd->bsgd', mixed, w_map).reshape(B, S, D_MODEL)
    return y * scale, ext[:, -POOL_HIST:]


def layer(x, pool_buf, conv_buf, ffn_buf, pos, norm_mix_g, w_in, b_gate, w_pool_map,
          pool_scale, conv_w, w_conv_out, w_o, norm_ffn_g, w_up, ffn_conv_w, ffn_conv_b, w_down):
    h = rmsnorm(x, norm_mix_g)
    proj = jnp.einsum('bsd,de->bse', h, w_in)
    o = 0
    u_pool = proj[..., o:o + POOL_WIDTH]; o += POOL_WIDTH
    gb = proj[..., o:o + CONV_WIDTH]; o += CONV_WIDTH
    gc = proj[..., o:o + CONV_WIDTH]; o += CONV_WIDTH
    v = proj[..., o:o + CONV_WIDTH]; o += CONV_WIDTH
    gate_logits = proj[..., o:o + N_BRANCH * D_MODEL] + b_gate
    y_a, new_pool = pool_mixer(u_pool, pool_buf, pos, w_pool_map, pool_scale)
    cv, new_conv = causal_dwconv(gc * v, conv_buf, conv_w)
    y_b = jnp.einsum('bsc,cd->bsd', gb * cv, w_conv_out)
    gates = jax.nn.sigmoid(gate_logits.astype(jnp.float32)).astype(x.dtype)
    merged = gates[..., :D_MODEL] * y_a + gates[..., D_MODEL:] * y_b
    x = x + jnp.einsum('bsd,de->bse', merged, w_o)
    h = rmsnorm(x, norm_ffn_g)
    up = jnp.einsum('bsd,df->bsf', h, w_up)
    upc, new_ffn = causal_dwconv(up, ffn_buf, ffn_conv_w)
    upc = upc + ffn_conv_b
    act = jax.nn.silu(upc[..., D_FF:]) * upc[..., :D_FF]
    x = x + jnp.einsum('bsf,fd->bsd', act, w_down)
    return x, new_pool, new_conv, new_ffn


def setup_inputs(seed: int = 0) -> dict:
    key = jax.random.key(seed)
    ks = jax.random.split(key, 24)
    f32 = jnp.float32
    nrm = lambda k, shape, s: (jax.random.normal(k, shape, f32) * s)
    return {
        "x_prompt": nrm(ks[0], (BATCH, SEQ, D_MODEL), 1.0),
        "x_sample": nrm(ks[1], (DEC_BATCH, DEC_SEQ, D_MODEL), 1.0),
        "state_pool": nrm(ks[2], (DEPTH, DEC_BATCH, POOL_HIST, POOL_WIDTH), 1.0),
        "state_conv": nrm(ks[3], (DEPTH, DEC_BATCH, CONV_K - 1, CONV_WIDTH), 1.0),
        "state_ffn": nrm(ks[4], (DEPTH, DEC_BATCH, CONV_K - 1, 2 * D_FF), 1.0),
        "norm_mix_g": 1.0 + nrm(ks[5], (DEPTH, D_MODEL), 0.05),
        "w_in": nrm(ks[6], (DEPTH, D_MODEL, IN_WIDTH), D_MODEL ** -0.5),
        "b_gate": nrm(ks[7], (DEPTH, N_BRANCH * D_MODEL), 0.05),
        "w_pool_map": nrm(ks[8], (DEPTH, POOL_GROUPS, POOL_GROUP_DIM, POOL_OUT_GROUP), POOL_GROUP_DIM ** -0.5),
        "pool_scale": 1.0 + nrm(ks[9], (DEPTH, D_MODEL), 0.05),
        "conv_w": nrm(ks[10], (DEPTH, CONV_K, CONV_WIDTH), CONV_K ** -0.5),
        "w_conv_out": nrm(ks[11], (DEPTH, CONV_WIDTH, D_MODEL), CONV_WIDTH ** -0.5),
        "w_o": nrm(ks[12], (DEPTH, D_MODEL, D_MODEL), D_MODEL ** -0.5),
        "norm_ffn_g": 1.0 + nrm(ks[13], (DEPTH, D_MODEL), 0.05),
        "w_up": nrm(ks[14], (DEPTH, D_MODEL, 2 * D_FF), D_MODEL ** -0.5),
        "ffn_conv_w": nrm(ks[15], (DEPTH, CONV_K, 2 * D_FF), CONV_K ** -0.5),
        "ffn_conv_b": nrm(ks[16], (DEPTH, 2 * D_FF), 0.02),
        "w_down": nrm(ks[17], (DEPTH, D_FF, D_MODEL), D_FF ** -0.5),
        "final_norm_g": 1.0 + nrm(ks[18], (D_MODEL,), 0.05),
    }


def reference(x_prompt, x_sample, state_pool, state_conv, state_ffn, norm_mix_g, w_in, b_gate,
              w_pool_map, pool_scale, conv_w, w_conv_out, w_o, norm_ffn_g, w_up, ffn_conv_w,
              ffn_conv_b, w_down, final_norm_g):
    S_p = x_prompt.shape[1]
    S_s = x_sample.shape[1]
    pos_p = jnp.arange(S_p, dtype=jnp.int32)
    pos_s = PAST_LEN + jnp.arange(S_s, dtype=jnp.int32)
    zp_pool = jnp.zeros((x_prompt.shape[0], POOL_HIST, POOL_WIDTH), x_prompt.dtype)
    zp_conv = jnp.zeros((x_prompt.shape[0], CONV_K - 1, CONV_WIDTH), x_prompt.dtype)
    zp_ffn = jnp.zeros((x_prompt.shape[0], CONV_K - 1, 2 * D_FF), x_prompt.dtype)

    xp, xs = x_prompt, x_sample
    pp, cp, fp, ps, cs_, fs = [], [], [], [], [], []
    for l in range(DEPTH):
        params = (norm_mix_g[l], w_in[l], b_gate[l], w_pool_map[l], pool_scale[l], conv_w[l],
                  w_conv_out[l], w_o[l], norm_ffn_g[l], w_up[l], ffn_conv_w[l], ffn_conv_b[l], w_down[l])
        xp, a, b, c = layer(xp, zp_pool, zp_conv, zp_ffn, pos_p, *params)
        pp.append(a); cp.append(b); fp.append(c)
        xs, a, b, c = layer(xs, state_pool[l], state_conv[l], state_ffn[l], pos_s, *params)
        ps.append(a); cs_.append(b); fs.append(c)

    y_prompt = rmsnorm(xp, final_norm_g)
    y_sample = rmsnorm(xs, final_norm_g)
    return (y_prompt, y_sample, jnp.stack(pp), jnp.stack(cp), jnp.stack(fp),
            jnp.stack(ps), jnp.stack(cs_), jnp.stack(fs))
```

```python
import numpy as np
import ml_dtypes
from contextlib import ExitStack
import concourse.bass as bass
import concourse.mybir as mybir
from concourse.bass_utils import run_bass_kernel_spmd

F32 = mybir.dt.float32
BF16 = mybir.dt.bfloat16
ALU = mybir.AluOpType
AF = mybir.ActivationFunctionType

D = 1024
DFF = 2816
NL = 2
NPIECE = 31
NSLOT = 6
SLOTW = 4096
ROWP = 4160
NCAST = 62
EPS = 1e-6
N_CORES = 8
NPT = 8

BG0 = 0
PS0 = BG0 + NL * 16
CW0 = PS0 + NL * 8
FW0 = CW0 + NL * 3 * 4
FB0 = FW0 + NL * 3 * 44
NCOL = FB0 + NL * 44


def piece_width(p):
    if 4 <= p < 12:
        return 2688
    if p in (27, 30):
        return 3072
    return 4096


def _kmajor(W):
    K, C = W.shape
    return W.reshape(K // 128, 128, C).transpose(1, 0, 2).reshape(128, -1)


def make_pieces(w_in, w_pool_map, w_conv_out, w_o, w_up, w_down):
    out = np.zeros((NL * NPIECE, 128, ROWP), np.float32)
    for l in range(NL):
        ps = []
        for j in range(4):
            ps.append(_kmajor(w_in[l][:, j * 512:(j + 1) * 512]))
        for d in range(8):
            ga = _kmajor(w_in[l][:, 2048 + d * 128:2048 + (d + 1) * 128])
            gb = _kmajor(w_in[l][:, 3072 + d * 128:3072 + (d + 1) * 128])
            co = _kmajor(w_conv_out[l][:, d * 128:(d + 1) * 128])
            pm = w_pool_map[l][d // 2][:, (d % 2) * 128:(d % 2 + 1) * 128]
            ps.append(np.concatenate([ga, gb, co, pm], axis=1))
        for h in range(2):
            ps.append(_kmajor(w_o[l][:, h * 512:(h + 1) * 512]))
        for j in range(11):
            cols = np.concatenate([
                np.arange((2 * j) * 128, (2 * j + 2) * 128),
                DFF + np.arange((2 * j) * 128, (2 * j + 2) * 128)])
            ps.append(_kmajor(w_up[l][:, cols]))
        for h in range(2):
            for r in range(3):
                ps.append(_kmajor(w_down[l][r * 1024:min((r + 1) * 1024, DFF), h * 512:(h + 1) * 512]))
        assert len(ps) == NPIECE
        for i, p in enumerate(ps):
            assert p.shape[1] == piece_width(i), (i, p.shape)
            out[l * NPIECE + i, :, :p.shape[1]] = p
    return out


def make_pcol(b_gate, pool_scale, conv_w, ffn_conv_w, ffn_conv_b):
    pc = np.zeros((128, NCOL), np.float32)
    for l in range(NL):
        pc[:, BG0 + l * 16:BG0 + (l + 1) * 16] = b_gate[l].reshape(16, 128).T
        pc[:, PS0 + l * 8:PS0 + (l + 1) * 8] = pool_scale[l].reshape(8, 128).T
        for k in range(3):
            o = CW0 + (l * 3 + k) * 4
            pc[:, o:o + 4] = conv_w[l][k].reshape(4, 128).T
            o = FW0 + (l * 3 + k) * 44
            pc[:, o:o + 44] = ffn_conv_w[l][k].reshape(44, 128).T
        pc[:, FB0 + l * 44:FB0 + (l + 1) * 44] = ffn_conv_b[l].reshape(44, 128).T
    return pc


class Eng:
    def __init__(self, e, sem, kind):
        self.e = e
        self.sem = sem
        self.kind = kind
        self.n = 0
        self.waited = {}

    def wait(self, sem, val):
        if self.waited.get(sem, 0) >= val:
            return
        self.e.wait_ge(sem, val)
        self.waited[sem] = val


class Trk:
    def __init__(self):
        self.lw = {}
        self.rs = {}

    def sync(self, E, reads, writes):
        need = {}

        def add(tok, raw):
            sem, val = tok
            if sem is E.sem:
                if E.kind == 'pe':
                    return
                if E.kind in ('act', 'dve', 'pool') and not raw:
                    return
            if need.get(sem, 0) < val:
                need[sem] = val
        for k in reads:
            t = self.lw.get(k)
            if t is not None:
                add(t, True)
        for k in writes:
            t = self.lw.get(k)
            if t is not None:
                add(t, False)
            for s, v in self.rs.get(k, {}).items():
                add((s, v), False)
        for s, v in need.items():
            E.wait(s, v)

    def commit(self, tok, reads, writes):
        for k in writes:
            self.lw[k] = tok
            self.rs[k] = {}
        for k in reads:
            d = self.rs.setdefault(k, {})
            if d.get(tok[0], 0) < tok[1]:
                d[tok[0]] = tok[1]


class _Stop(Exception):
    pass


def build_program(npt=NPT, with_sample=True, dbg=None):
    nc = bass.Bass("TRN2", target_bir_lowering=False)
    xp = nc.dram_tensor("xp", [npt * 512, D], F32, kind="ExternalInput").ap()
    xs = nc.dram_tensor("xs", [128, D], F32, kind="ExternalInput").ap()
    wp = nc.dram_tensor("wp", [NL * NPIECE, 128, ROWP], F32, kind="ExternalInput").ap()
    pcol_d = nc.dram_tensor("pcol", [128, NCOL], F32, kind="ExternalInput").ap()
    gall_d = nc.dram_tensor("gall", [5, D], F32, kind="ExternalInput").ap()
    ident_d = nc.dram_tensor("ident", [128, 128], BF16, kind="ExternalInput").ap()
    invc_d = nc.dram_tensor("invcnt", [128, 4, 16], F32, kind="ExternalInput").ap()
    sp_in = nc.dram_tensor("sp_in", [128, NL * 4 * 2 * 15], F32, kind="ExternalInput").ap()
    sc_in = nc.dram_tensor("sc_in", [128, NL * 4 * 2 * 2], F32, kind="ExternalInput").ap()
    sf_in = nc.dram_tensor("sf_in", [128, NL * 44 * 2 * 2], F32, kind="ExternalInput").ap()
    yp = nc.dram_tensor("yp", [npt * 512, D], F32, kind="ExternalOutput").ap()
    ys = nc.dram_tensor("ys", [128, D], F32, kind="ExternalOutput").ap()
    o_state = {
        ('p', 'pool'): nc.dram_tensor("o_pool_p", [128, NL * 4 * 1 * 15], F32, kind="ExternalOutput").ap(),
        ('p', 'conv'): nc.dram_tensor("o_conv_p", [128, NL * 4 * 1 * 2], F32, kind="ExternalOutput").ap(),
        ('p', 'ffn'): nc.dram_tensor("o_ffn_p", [128, NL * 44 * 1 * 2], F32, kind="ExternalOutput").ap(),
        ('s', 'pool'): nc.dram_tensor("o_pool_s", [128, NL * 4 * 2 * 15], F32, kind="ExternalOutput").ap(),
        ('s', 'conv'): nc.dram_tensor("o_conv_s", [128, NL * 4 * 2 * 2], F32, kind="ExternalOutput").ap(),
        ('s', 'ffn'): nc.dram_tensor("o_ffn_s", [128, NL * 44 * 2 * 2], F32, kind="ExternalOutput").ap(),
    }
    wsc = nc.dram_tensor("wsc", [NL * NPIECE, 128, ROWP], BF16).ap()

    es = ExitStack()
    with es:
        def sb(name, shape, dt):
            return es.enter_context(nc.sbuf_tensor("sb_" + name, shape, dt))

        def newsem(name):
            return es.enter_context(nc.semaphore(name))

        xbuf = [sb(f"xbuf{i}", [128, 4, D], F32) for i in range(2)]
        hbuf = [sb(f"hbuf{i}", [128, D], BF16) for i in range(4)]
        junk = sb("junk", [128, D], BF16)
        hT = sb("hT", [128, 8, 512], BF16)
        slots = [sb(f"slot{i}", [128, SLOTW], BF16) for i in range(NSLOT)]
        gall = sb("gall", [128, 5, D], F32)
        pcol = sb("pcol", [128, NCOL], F32)
        ident = sb("ident", [128, 128], BF16)
        invc = sb("invc", [128, 4, 16], F32)
        nhalf = sb("nhalf", [128, 1], F32)
        NST = 8
        ssb = sb("ssb", [128, NST], F32)
        msb = sb("msb", [128, NST], F32)
        rsb = sb("rsb", [128, NST], F32)
        fix16 = sb("fix16", [128, 16], F32)
        hcbuf = sb("hcbuf", [128, 44 * 2 * 2], F32)
        tmp44 = sb("tmp44", [128, 44], F32)
        HSdims = {('p', 'pool'): (4, 1, 15), ('p', 'conv'): (4, 1, 2), ('p', 'ffn'): (44, 1, 2),
                  ('s', 'pool'): (4, 2, 15), ('s', 'conv'): (4, 2, 2), ('s', 'ffn'): (44, 2, 2)}
        HS2 = {k: sb("h%s_%s" % k, [128, NL * v[0] * v[1] * v[2]], F32) for k, v in HSdims.items()}
        HS = {k: HS2[k][:].rearrange("p (l c s r) -> p l c s r", l=NL, c=v[0], s=v[1]) for k, v in HSdims.items()}
        UW = 528
        CVW = 516
        UPW = 516
        arenaA = sb("arenaA", [128, 25600], mybir.dt.uint8)
        arenaB = sb("arenaB", [128, 37376], mybir.dt.uint8)

        def carve(arena, off, nelem, dt):
            nb = nelem * (4 if dt == F32 else 2)
            v = arena[:, off:off + nb].bitcast(dt)
            return v, off + nb
        o = 0
        ubuf, o = carve(arenaA, o, 4 * UW, F32)
        gbbuf, o = carve(arenaA, o, 4 * 512, F32)
        cvin, o = carve(arenaA, o, 4 * CVW, F32)
        assert o <= 25600
        actT, _ = carve(arenaA, 0, 22 * 512, BF16)
        o = 0
        sga = [None, None]
        sgb = [None, None]
        for i in range(2):
            sga[i], o = carve(arenaB, o, 512, F32)
            sgb[i], o = carve(arenaB, o, 512, F32)
        mergedT, o = carve(arenaB, o, 8 * 512, BF16)
        mixedT, o = carve(arenaB, o, 4 * 512, BF16)
        gcvT, o = carve(arenaB, o, 4 * 512, BF16)
        ptmp = [None, None]
        for i in range(2):
            ptmp[i], o = carve(arenaB, o, UW, F32)
        gctmp = [None, None]
        cvt = [None, None]
        for i in range(2):
            gctmp[i], o = carve(arenaB, o, 512, F32)
            cvt[i], o = carve(arenaB, o, 512, F32)
        assert o <= 37376, o
        o = 0
        upbuf = []
        tbuf = []
        for i in range(8):
            a, o = carve(arenaB, o, UPW, F32)
            upbuf.append(a)
        for i in range(8):
            a, o = carve(arenaB, o, 512, F32)
            tbuf.append(a)
        assert o <= 37376, o

        banks = [es.enter_context(nc.psum_tensor(f"bank{i}", [128, 512], F32)) for i in range(8)]

        s_pe, s_act, s_dve, s_pool = newsem("s_pe"), newsem("s_act"), newsem("s_dve"), newsem("s_pool")
        s_slot = [newsem(f"s_slot{i}") for i in range(NSLOT)]
        s_cast = [newsem(f"s_cast{i}") for i in range(NCAST)]
        s_xl = [newsem(f"s_xl{i}") for i in range(2)]
        s_ys = [newsem(f"s_ys{i}") for i in range(2)]
        s_const = newsem("s_const")
        s_out = newsem("s_out")

        PE = Eng(nc.tensor, s_pe, 'pe')
        ACT = Eng(nc.scalar, s_act, 'act')
        DVE = Eng(nc.vector, s_dve, 'dve')
        POOL = Eng(nc.gpsimd, s_pool, 'pool')
        SYNC = Eng(nc.sync, None, 'dma')
        trk = Trk()
        semcnt = {}

        def op(E, fn, reads=(), writes=()):
            trk.sync(E, reads, writes)
            ins = fn(E.e)
            E.n += 1
            ins.then_inc(E.sem, 1)
            tok = (E.sem, E.n)
            trk.commit(tok, reads, writes)
            return tok

        def dma(E, sem, fn, reads=(), writes=()):
            rk = list(reads)
            wk = list(writes) + [("sem", sem)]
            trk.sync(E, rk, wk)
            ins = fn(E.e)
            ins.then_inc(sem, 16)
            semcnt[sem] = semcnt.get(sem, 0) + 16
            tok = (sem, semcnt[sem])
            trk.commit(tok, rk, wk)
            return tok

        def pe_group(fns, reads, writes):
            trk.sync(PE, reads, writes)
            ins = None
            for f in fns:
                ins = f(PE.e)
            PE.n += 1
            ins.then_inc(PE.sem, 1)
            tok = (PE.sem, PE.n)
            trk.commit(tok, reads, writes)
            return tok

        bank_ctr = [0]

        def next_bank():
            b = bank_ctr[0] % 8
            bank_ctr[0] += 1
            return b

        stat_ctr = [0]
        defT = []

        op(DVE, lambda e: e.memset(HS2[('p', 'pool')][:], 0.0), writes=[('H', 'p', 'pool')])
        op(DVE, lambda e: e.memset(HS2[('p', 'conv')][:], 0.0), writes=[('H', 'p', 'conv')])
        op(DVE, lambda e: e.memset(HS2[('p', 'ffn')][:], 0.0), writes=[('H', 'p', 'ffn')])
        op(DVE, lambda e: e.memset(nhalf[:], -0.5), writes=["nhalf"])

        tiles = []
        for i in range(npt):
            tiles.append(dict(kind='p', idx=i, ntok=512, nseg=1, L=512, nsub=4, first=(i == 0)))
        if with_sample:
            tiles.append(dict(kind='s', idx=0, ntok=128, nseg=2, L=64, nsub=1, first=False))
        NT = len(tiles)

        cast_next = [0]

        def issue_casts(upto):
            upto = min(upto, NL * NPIECE)
            while cast_next[0] < upto:
                p = cast_next[0]
                w = piece_width(p % NPIECE)
                dma(POOL, s_cast[p % NCAST],
                    lambda e, p=p, w=w: e.dma_start(out=wsc[p, :, 0:w], in_=wp[p, :, 0:w]),
                    reads=[], writes=[("wsc", p)])
                cast_next[0] += 1

        load_ctr = [0]
        pending_sync = []

        def load_piece(l, p):
            gi = l * NPIECE + p
            w = piece_width(p)
            n = load_ctr[0]
            si = n % NSLOT
            for item in list(pending_sync):
                if item[0] <= n:
                    pending_sync.remove(item)
                    item[1]()
            dma(SYNC, s_slot[si],
                lambda e: e.dma_start(out=slots[si][:, 0:w], in_=wsc[gi, :, 0:w]),
                reads=[("wsc", gi)], writes=[("slot", si)])
            load_ctr[0] += 1
            return si

        sched = [(t, l, p) for t in range(NT) for l in range(NL) for p in range(NPIECE)]
        sched_pos = [0]
        slot_of = {}

        def prefetch(upto_idx):
            while sched_pos[0] < min(upto_idx, len(sched)):
                t, l, p = sched[sched_pos[0]]
                slot_of[(t, l, p)] = load_piece(l, p)
                sched_pos[0] += 1

        def use_piece(t, l, p, oldest=None):
            gidx = (t * NL + l) * NPIECE + p
            gold = gidx if oldest is None else (t * NL + l) * NPIECE + oldest
            prefetch(gidx + 1)
            prefetch(gold + NSLOT)
            return slot_of[(t, l, p)]

        def x_load(ti):
            tl = tiles[ti]
            b = ti % 2
            if tl['kind'] == 'p':
                src = xp[tl['idx'] * 512:(tl['idx'] + 1) * 512, :].rearrange("(s p) d -> p s d", p=128)
                dst = xbuf[b][:, 0:4, :]
            else:
                src = xs
                dst = xbuf[b][:, 0, :]
            dma(SYNC, s_xl[b], lambda e: e.dma_start(out=dst, in_=src),
                reads=[], writes=[("x", b, s) for s in range(4)])

        def y_store(ti):
            tl = tiles[ti]
            b = ti % 2
            if tl['kind'] == 'p':
                dst = yp[tl['idx'] * 512:(tl['idx'] + 1) * 512, :].rearrange("(s p) d -> p s d", p=128)
                src = xbuf[b][:, 0:4, :]
            else:
                dst = ys
                src = xbuf[b][:, 0, :]
            dma(SYNC, s_ys[b], lambda e: e.dma_start(out=dst, in_=src),
                reads=[("x", b, s) for s in range(tl['nsub'])], writes=[])

        def norm_B(ti, s):
            b = ti % 2
            i = stat_ctr[0]
            stat_ctr[0] += 1
            c = i % NST
            par = i % 4
            xs_ = xbuf[b][:, s, :]
            op(ACT, lambda e: e.activation(out=junk[:], in_=xs_, func=AF.Square, accum_out=ssb[:, c:c + 1]),
               reads=[("x", b, s)], writes=["junk", ("ss", c)])
            return c, par

        def norm_C(c, no_pool=False):
            op(DVE, lambda e: e.tensor_scalar(out=msb[:, c:c + 1], in0=ssb[:, c:c + 1], scalar1=1.0 / D, scalar2=EPS,
                                              op0=ALU.mult, op1=ALU.add),
               reads=[("ss", c)], writes=[("ms", c)])
            if no_pool:
                op(ACT, lambda e: e.activation(out=msb[:, c:c + 1], in_=msb[:, c:c + 1], func=AF.Sqrt),
                   reads=[("ms", c)], writes=[("ms", c)])
                op(DVE, lambda e: e.reciprocal(out=rsb[:, c:c + 1], in_=msb[:, c:c + 1]),
                   reads=[("ms", c)], writes=[("rs", c)])
            else:
                op(POOL, lambda e: e.tensor_tensor(out=rsb[:, c:c + 1], in0=msb[:, c:c + 1], in1=nhalf[:], op=ALU.pow),
                   reads=[("ms", c), "nhalf"], writes=[("rs", c)])

        def norm_D(ti, s, gidx, c, par):
            b = ti % 2
            xs_ = xbuf[b][:, s, :]
            op(DVE, lambda e: e.scalar_tensor_tensor(out=hbuf[par][:], in0=xs_, scalar=rsb[:, c:c + 1], in1=gall[:, gidx, :],
                                                     op0=ALU.mult, op1=ALU.mult),
               reads=[("x", b, s), ("rs", c), "gall"], writes=[("h", par)])

        def norm_D_final(ti, s, c):
            b = ti % 2
            xs_ = xbuf[b][:, s, :]
            op(DVE, lambda e: e.scalar_tensor_tensor(out=xs_, in0=xs_, scalar=rsb[:, c:c + 1], in1=gall[:, 4, :],
                                                     op0=ALU.mult, op1=ALU.mult),
               reads=[("x", b, s), ("rs", c), "gall"], writes=[("x", b, s)])

        def norm_pipeline(ti, nsub_, gidx, emit_A=None, final=False, no_pool=False, defer=0, mid_hook=None):
            st = {}
            pending = []
            e_done = set()

            def stage_E(s_):
                if s_ in e_done or not (0 <= s_ < nsub_) or final:
                    return
                e_done.add(s_)
                if s_ >= nsub_ - defer:
                    pending.append((s_, st[s_][1]))
                else:
                    transpose_sub(s_, st[s_][1])

            for i in range(nsub_ + 3):
                def stage_D(i=i):
                    if 0 <= i - 2 < nsub_:
                        if final:
                            norm_D_final(ti, i - 2, st[i - 2][0])
                        else:
                            norm_D(ti, i - 2, gidx, *st[i - 2])
                d_done = False
                if i < nsub_:
                    last_A = (i == nsub_ - 1)
                    if emit_A is not None:
                        if mid_hook == 'always' or (mid_hook == 'last' and last_A):
                            emit_A(i, stage_D)
                            d_done = True
                        else:
                            emit_A(i)
                    if last_A and d_done and defer > 0 and not final:
                        for s_ in (i - 3, i - 2):
                            if s_ < nsub_ - defer:
                                stage_E(s_)
                    st[i] = norm_B(ti, i)
                if 0 <= i - 1 < nsub_:
                    norm_C(st[i - 1][0], no_pool)
                if not d_done:
                    stage_D()
                stage_E(i - 3)
            return pending

        def transpose_sub(s, par):
            bk = next_bank()
            pT = banks[bk][:].bitcast(BF16)
            fns = [(lambda e, k=k: e.transpose(out=pT[:, k * 128:(k + 1) * 128], in_=hbuf[par][:, k * 128:(k + 1) * 128],
                                               identity=ident[:])) for k in range(8)]
            pe_group(fns, reads=[("h", par), "ident"], writes=[("ps", bk)])
            op(ACT, lambda e: e.copy(out=hT[:, :, s * 128:(s + 1) * 128],
                                     in_=pT.rearrange("p (k t) -> p k t", k=8)),
               reads=[("ps", bk)], writes=[("hT", s)])

        def layer(ti, l, hoist=None):
            tl = tiles[ti]
            kind, ntok, nseg, L, nsub, first = tl['kind'], tl['ntok'], tl['nseg'], tl['L'], tl['nsub'], tl['first']
            b = ti % 2
            hTk = [("hT", s) for s in range(nsub)]

            def seg(ap2d):
                return ap2d[:, 0:ntok].rearrange("p (s l) -> p s l", s=nseg)

            def pc(i):
                return pcol[:, i:i + 1]

            Hpool = HS[(kind, 'pool')]
            Hconv = HS[(kind, 'conv')]
            Hffn = HS[(kind, 'ffn')]
            kHp, kHc, kHf = ('H', kind, 'pool'), ('H', kind, 'conv'), ('H', kind, 'ffn')
            W = 15 + L
            U4 = ubuf[:, 0:4 * nseg * W].rearrange("p (g s w) -> p g s w", g=4, s=nseg)
            CW = 2 + L
            C4 = cvin[:, 0:4 * nseg * CW].rearrange("p (g s w) -> p g s w", g=4, s=nseg)

            op(DVE, lambda e: e.tensor_copy(out=U4[:, :, :, 0:15], in_=Hpool[:, l]),
               reads=[kHp], writes=[("u", g) for g in range(4)])
            op(DVE, lambda e: e.tensor_copy(out=C4[:, :, :, 0:2], in_=Hconv[:, l]),
               reads=[kHc], writes=[("cvin", g) for g in range(4)])

            def proj_group(si, c):
                bk = next_bank()
                Wt = slots[si]
                fns = [(lambda e, k=k: e.matmul(banks[bk][:, 0:ntok], lhsT=Wt[:, k * 512 + c * 128:k * 512 + (c + 1) * 128],
                                                rhs=hT[:, k, 0:ntok], start=(k == 0), stop=(k == 7))) for k in range(8)]
                pe_group(fns, reads=[("slot", si)] + hTk, writes=[("ps", bk)])
                return bk

            si = use_piece(ti, l, 0)

            def half_group(si_, bk, col0, t0, t1, keys):
                Wt_ = slots[si_]
                fns = [(lambda e, k=k: e.matmul(banks[bk][:, t0:t1], lhsT=Wt_[:, k * 512 + col0:k * 512 + col0 + 128],
                                                rhs=hT[:, k, t0:t1], start=(k == 0), stop=(k == 7))) for k in range(8)]
                pe_group(fns, reads=[("slot", si_)] + keys, writes=[("ps", bk)])

            if defT:
                ubk = [next_bank() for _ in range(4)]
                for qt in range(2):
                    for g in range(4):
                        half_group(si, ubk[g], g * 128, qt * 128, (qt + 1) * 128, [("hT", qt)])
                for t_ in defT:
                    transpose_sub(*t_)
                del defT[:]
                for g in range(4):
                    half_group(si, ubk[g], g * 128, 256, 384, [("hT", 2)])
                for g in range(4):
                    half_group(si, ubk[g], g * 128, 384, 512, [("hT", 3)])
                    op(ACT, lambda e, g=g: e.copy(out=U4[:, g, :, 15:15 + L], in_=seg(banks[ubk[g]])),
                       reads=[("ps", ubk[g])], writes=[("u", g)])
            else:
                for g in range(4):
                    bk = proj_group(si, g)
                    op(ACT, lambda e, g=g, bk=bk: e.copy(out=U4[:, g, :, 15:15 + L], in_=seg(banks[bk])),
                       reads=[("ps", bk)], writes=[("u", g)])
            op(DVE, lambda e: e.tensor_copy(out=Hpool[:, l], in_=U4[:, :, :, L:L + 15]),
               reads=[("u", g) for g in range(4)], writes=[kHp])
            cur = {}
            for g in range(4):
                cur[g] = None
            PT = [ptmp[i][:, 0:nseg * W].rearrange("p (s w) -> p s w", s=nseg) for i in range(2)]
            for g in range(4):
                win = 2 << g
                Ug = U4[:, g]
                src = Ug
                src_key = ("u", g)
                lo_prev = -15
                nlev = g + 1
                for j in range(1, nlev + 1):
                    half = 1 << (j - 1)
                    lo = -(win - (1 << j))
                    dst = PT[(j - 1) % 2]
                    a0 = 15 + lo
                    op(DVE, lambda e, dst=dst, src=src, a0=a0, half=half: e.tensor_tensor(
                        out=dst[:, :, a0:15 + L], in0=src[:, :, a0:15 + L], in1=src[:, :, a0 - half:15 + L - half], op=ALU.add),
                       reads=[src_key], writes=[("ptmp", (j - 1) % 2)])
                    src = dst
                    src_key = ("ptmp", (j - 1) % 2)
                mg = seg(mixedT[:, g * 512:(g + 1) * 512])
                op(DVE, lambda e, src=src, Ug=Ug, mg=mg, win=win: e.scalar_tensor_tensor(
                    out=mg, in0=src[:, :, 15:15 + L], scalar=1.0 / win, in1=Ug[:, :, 15:15 + L],
                    op0=ALU.mult, op1=ALU.subtract),
                   reads=[src_key, ("u", g)], writes=[("mixedT", g)])
                if first:
                    op(DVE, lambda e, src=src, g=g: e.tensor_tensor(out=fix16[:], in0=src[:, 0, 15:31], in1=invc[:, g, :], op=ALU.mult),
                       reads=[src_key, "invc"], writes=["fix16"])
                    op(DVE, lambda e, Ug=Ug, g=g: e.tensor_tensor(out=mixedT[:, g * 512:g * 512 + 16], in0=fix16[:], in1=Ug[:, 0, 15:31],
                                                                 op=ALU.subtract),
                       reads=["fix16", ("u", g)], writes=[("mixedT", g)])

            si = use_piece(ti, l, 1)
            for c in range(4):
                bk = proj_group(si, c)
                op(ACT, lambda e, c=c, bk=bk: e.copy(out=gbbuf[:, c * 512:c * 512 + ntok], in_=banks[bk][:, 0:ntok]),
                   reads=[("ps", bk)], writes=[("gb", c)])
            si_gc = use_piece(ti, l, 2)
            si_v = use_piece(ti, l, 3, oldest=2)
            for c in range(4):
                bk = proj_group(si_gc, c)
                op(ACT, lambda e, c=c, bk=bk: e.copy(out=gctmp[c % 2][:, 0:ntok], in_=banks[bk][:, 0:ntok]),
                   reads=[("ps", bk)], writes=[("gctmp", c % 2)])
                bk = proj_group(si_v, c)
                op(DVE, lambda e, c=c, bk=bk: e.tensor_tensor(out=C4[:, c, :, 2:2 + L], in0=seg(banks[bk]), in1=seg(gctmp[c % 2]), op=ALU.mult),
                   reads=[("ps", bk), ("gctmp", c % 2)], writes=[("cvin", c)])
                cw = lambda k: pc(CW0 + (l * 3 + k) * 4 + c)
                ct = seg(cvt[c % 2])
                op(ACT, lambda e, c=c, ct=ct, cw=cw: e.activation(out=ct, in_=C4[:, c, :, 0:L], func=AF.Identity, scale=cw(0)),
                   reads=[("cvin", c), "pcol"], writes=[("cvt", c % 2)])
                op(DVE, lambda e, c=c, ct=ct, cw=cw: e.scalar_tensor_tensor(out=ct, in0=C4[:, c, :, 1:1 + L], scalar=cw(1), in1=ct,
                                                                           op0=ALU.mult, op1=ALU.add),
                   reads=[("cvin", c), ("cvt", c % 2), "pcol"], writes=[("cvt", c % 2)])
                op(DVE, lambda e, c=c, ct=ct, cw=cw: e.scalar_tensor_tensor(out=ct, in0=C4[:, c, :, 2:2 + L], scalar=cw(2), in1=ct,
                                                                           op0=ALU.mult, op1=ALU.add),
                   reads=[("cvin", c), ("cvt", c % 2), "pcol"], writes=[("cvt", c % 2)])
                op(DVE, lambda e, c=c: e.tensor_tensor(out=gcvT[:, c * 512:c * 512 + ntok], in0=gbbuf[:, c * 512:c * 512 + ntok],
                                                       in1=cvt[c % 2][:, 0:ntok], op=ALU.mult),
                   reads=[("gb", c), ("cvt", c % 2)], writes=[("gcvT", c)])
            op(DVE, lambda e: e.tensor_copy(out=Hconv[:, l], in_=C4[:, :, :, L:L + 2]),
               reads=[("cvin", g) for g in range(4)], writes=[kHc])

            if dbg == (ti, l, 'M5'):
                raise _Stop()
            def m7_logits(d):
                si = use_piece(ti, l, 4 + d, oldest=(4 + d - 1 if d > 0 else None))
                Wt = slots[si]
                bA, bB = next_bank(), next_bank()
                fns = [(lambda e, k=k: e.matmul(banks[bA][:, 0:ntok], lhsT=Wt[:, k * 128:(k + 1) * 128], rhs=hT[:, k, 0:ntok],
                                                start=(k == 0), stop=(k == 7))) for k in range(8)]
                pe_group(fns, reads=[("slot", si)] + hTk, writes=[("ps", bA)])
                fns = [(lambda e, k=k: e.matmul(banks[bB][:, 0:ntok], lhsT=Wt[:, 1024 + k * 128:1024 + (k + 1) * 128], rhs=hT[:, k, 0:ntok],
                                                start=(k == 0), stop=(k == 7))) for k in range(8)]
                pe_group(fns, reads=[("slot", si)] + hTk, writes=[("ps", bB)])
                pa = d % 2
                op(ACT, lambda e: e.activation(out=sga[pa][:, 0:ntok], in_=banks[bA][:, 0:ntok], func=AF.Sigmoid,
                                               bias=pc(BG0 + l * 16 + d)),
                   reads=[("ps", bA), "pcol"], writes=[("sga", pa)])
                op(ACT, lambda e: e.activation(out=sgb[pa][:, 0:ntok], in_=banks[bB][:, 0:ntok], func=AF.Sigmoid,
                                               bias=pc(BG0 + l * 16 + 8 + d)),
                   reads=[("ps", bB), "pcol"], writes=[("sgb", pa)])
                return si

            def m7_y(d, si):
                Wt = slots[si]
                g = d // 2
                pa = d % 2
                bC, bD = next_bank(), next_bank()
                fns = [lambda e: e.matmul(banks[bC][:, 0:ntok], lhsT=Wt[:, 2560:2688], rhs=mixedT[:, g * 512:g * 512 + ntok],
                                          start=True, stop=True)]
                pe_group(fns, reads=[("slot", si), ("mixedT", g)], writes=[("ps", bC)])
                fns = [(lambda e, k=k: e.matmul(banks[bD][:, 0:ntok], lhsT=Wt[:, 2048 + k * 128:2048 + (k + 1) * 128],
                                                rhs=gcvT[:, k * 512:k * 512 + ntok], start=(k == 0), stop=(k == 3))) for k in range(4)]
                pe_group(fns, reads=[("slot", si)] + [("gcvT", k) for k in range(4)], writes=[("ps", bD)])
                op(DVE, lambda e: e.scalar_tensor_tensor(out=sga[pa][:, 0:ntok], in0=banks[bC][:, 0:ntok], scalar=pc(PS0 + l * 8 + d),
                                                         in1=sga[pa][:, 0:ntok], op0=ALU.mult, op1=ALU.mult),
                   reads=[("ps", bC), ("sga", pa), "pcol"], writes=[("sga", pa)])
                op(DVE, lambda e: e.tensor_tensor(out=sgb[pa][:, 0:ntok], in0=banks[bD][:, 0:ntok], in1=sgb[pa][:, 0:ntok], op=ALU.mult),
                   reads=[("ps", bD), ("sgb", pa)], writes=[("sgb", pa)])
                op(DVE, lambda e: e.tensor_tensor(out=mergedT[:, d * 512:d * 512 + ntok], in0=sga[pa][:, 0:ntok], in1=sgb[pa][:, 0:ntok],
                                                  op=ALU.add),
                   reads=[("sga", pa), ("sgb", pa)], writes=[("mergedT", d)])

            m7si = {0: m7_logits(0)}
            for d in range(8):
                if d + 1 < 8:
                    m7si[d + 1] = m7_logits(d + 1)
                m7_y(d, m7si[d])

            if dbg == (ti, l, 'M7'):
                raise _Stop()
            if ti == 0:
                issue_casts(l * NPIECE + 14 + NCAST)
            si0 = use_piece(ti, l, 12)
            si1 = use_piece(ti, l, 13, oldest=12)
            def m8_A(s, mid=None):
                bk2 = [next_bank(), next_bank()]
                ksplits = [(0, 6), (6, 8)] if s == 0 else [(0, 8)]
                for (k0, k1) in ksplits:
                    for h, si in ((0, si0), (1, si1)):
                        bk = bk2[h]
                        Wt = slots[si]
                        fns = [(lambda e, k=k: e.matmul(banks[bk][:, :], lhsT=mergedT[:, k * 512 + s * 128:k * 512 + (s + 1) * 128],
                                                        rhs=Wt[:, k * 512:(k + 1) * 512], start=(k == 0), stop=(k == 7))) for k in range(k0, k1)]
                        pe_group(fns, reads=[("slot", si)] + [("mergedT", k) for k in range(k0, k1)], writes=[("ps", bk)])
                for h in range(2):
                    bk = bk2[h]
                    xsl = xbuf[b][:, s, h * 512:(h + 1) * 512]
                    op(DVE, lambda e, bk=bk, xsl=xsl: e.tensor_tensor(out=xsl, in0=banks[bk][:, :], in1=xsl, op=ALU.add),
                       reads=[("ps", bk), ("x", b, s)], writes=[("x", b, s)])
                    if h == 0 and mid is not None:
                        mid()
            defT.extend(norm_pipeline(ti, nsub, 2 * l + 1, emit_A=m8_A, defer=(2 if nsub == 4 else 0), mid_hook='always'))

            if dbg == (ti, l, 'M8'):
                raise _Stop()
            fcol = lambda k, ch: pc(FW0 + (l * 3 + k) * 44 + ch)
            fw_all = lambda k: pcol[:, FW0 + (l * 3 + k) * 44:FW0 + (l * 3 + k + 1) * 44]
            fb_all = pcol[:, FB0 + l * 44:FB0 + (l + 1) * 44]
            HCt = hcbuf[:, 0:44 * nseg * 2].rearrange("p (c s r) -> p c s r", c=44, s=nseg)
            Hl = Hffn[:, l]
            for sgi in range(nseg):
                a0, a1 = Hl[:, :, sgi, 0], Hl[:, :, sgi, 1]
                o0, o1 = HCt[:, :, sgi, 0], HCt[:, :, sgi, 1]
                op(DVE, lambda e: e.tensor_tensor(out=o0, in0=a0, in1=fw_all(0), op=ALU.mult), reads=[kHf, "pcol"], writes=["hc"])
                op(DVE, lambda e: e.tensor_tensor(out=tmp44[:], in0=a1, in1=fw_all(1), op=ALU.mult), reads=[kHf, "pcol"], writes=["tmp44"])
                op(DVE, lambda e: e.tensor_tensor(out=o1, in0=a1, in1=fw_all(0), op=ALU.mult), reads=[kHf, "pcol"], writes=["hc"])
                op(DVE, lambda e: e.tensor_tensor(out=o0, in0=o0, in1=tmp44[:], op=ALU.add), reads=["hc", "tmp44"], writes=["hc"])
                op(DVE, lambda e: e.tensor_tensor(out=o1, in0=o1, in1=fb_all, op=ALU.add), reads=["hc", "pcol"], writes=["hc"])
                op(DVE, lambda e: e.tensor_tensor(out=o0, in0=o0, in1=fb_all, op=ALU.add), reads=["hc", "pcol"], writes=["hc"])
            for j in range(11):
                si = use_piece(ti, l, 14 + j)
                Wt = slots[si]
                chs = [2 * j, 2 * j + 1, 22 + 2 * j, 22 + 2 * j + 1]
                info = []
                pre = None
                if j == 0 and defT:
                    pre = [next_bank() for _ in range(4)]
                    for qt in range(2):
                        for q in range(4):
                            half_group(si, pre[q], q * 128, qt * 128, (qt + 1) * 128, [("hT", qt)])
                    for t_ in defT:
                        transpose_sub(*t_)
                    del defT[:]
                    for q in range(4):
                        half_group(si, pre[q], q * 128, 256, 384, [("hT", 2)])
                for q in range(4):
                    ch = chs[q]
                    ui = (j % 2) * 4 + q
                    TB = seg(tbuf[ui])
                    if pre is not None:
                        bk = pre[q]
                        half_group(si, bk, q * 128, 384, 512, [("hT", 3)])
                    else:
                        bk = next_bank()
                        fns = [(lambda e, k=k: e.matmul(banks[bk][:, 0:ntok], lhsT=Wt[:, k * 512 + q * 128:k * 512 + (q + 1) * 128],
                                                        rhs=hT[:, k, 0:ntok], start=(k == 0), stop=(k == 7))) for k in range(8)]
                        pe_group(fns, reads=[("slot", si)] + hTk, writes=[("ps", bk)])
                    P3 = seg(banks[bk])
                    op(ACT, lambda e: e.copy(out=TB[:, :, 0:2], in_=HCt[:, ch]), reads=["hc"], writes=[("tb", ui)])
                    op(ACT, lambda e: e.activation(out=TB[:, :, 2:L], in_=P3[:, :, 0:L - 2], func=AF.Identity,
                                                   scale=fcol(0, ch), bias=pc(FB0 + l * 44 + ch)),
                       reads=[("ps", bk), "pcol"], writes=[("tb", ui)])
                    op(ACT, lambda e: e.copy(out=Hffn[:, l, ch], in_=P3[:, :, L - 2:L]), reads=[("ps", bk), "hc"], writes=[kHf])
                    info.append((ch, ui, TB, P3, bk))
                def tap1(ch, ui, TB, P3, bk):
                    op(DVE, lambda e: e.scalar_tensor_tensor(out=TB[:, :, 1:L], in0=P3[:, :, 0:L - 1], scalar=fcol(1, ch), in1=TB[:, :, 1:L],
                                                             op0=ALU.mult, op1=ALU.add),
                       reads=[("ps", bk), ("tb", ui), "pcol"], writes=[("tb", ui)])

                def tap2(ch, ui, TB, P3, bk):
                    op(DVE, lambda e: e.scalar_tensor_tensor(out=TB, in0=P3, scalar=fcol(2, ch), in1=TB, op0=ALU.mult, op1=ALU.add),
                       reads=[("ps", bk), ("tb", ui), "pcol"], writes=[("tb", ui)])
                for pr in ((0, 1), (2, 3)):
                    for q_ in pr:
                        tap1(*info[q_])
                    for q_ in pr:
                        tap2(*info[q_])
                for q in (2, 3):
                    ui = (j % 2) * 4 + q
                    op(ACT, lambda e: e.activation(out=tbuf[ui][:, 0:ntok], in_=tbuf[ui][:, 0:ntok], func=AF.Silu),
                       reads=[("tb", ui)], writes=[("tb", ui)])
                for q in range(2):
                    uv = (j % 2) * 4 + q
                    ug = uv + 2
                    ch = chs[q]
                    op(DVE, lambda e: e.tensor_tensor(out=actT[:, ch * 512:ch * 512 + ntok], in0=tbuf[uv][:, 0:ntok],
                                                      in1=tbuf[ug][:, 0:ntok], op=ALU.mult),
                       reads=[("tb", uv), ("tb", ug)], writes=[("actT", ch)])

            if dbg == (ti, l, 'F5'):
                raise _Stop()
            last_layer = (l == NL - 1)
            gnext = 2 * (l + 1) if not last_layer else 4
            pendq = []
            if ti == 0:
                issue_casts((l + 1) * NPIECE + NCAST)
            bks = [next_bank() for _ in range(nsub)]
            for r in range(3):
                si = use_piece(ti, l, 25 + r)
                Wt = slots[si]
                nk = 8 if r < 2 else 6
                for s in range(nsub):
                    bk = bks[s]
                    fns = [(lambda e, kk=kk: e.matmul(banks[bk][:, :],
                                                      lhsT=actT[:, (r * 8 + kk) * 512 + s * 128:(r * 8 + kk) * 512 + (s + 1) * 128],
                                                      rhs=Wt[:, kk * 512:(kk + 1) * 512],
                                                      start=(r == 0 and kk == 0), stop=(r == 2 and kk == nk - 1))) for kk in range(nk)]
                    pe_group(fns, reads=[("slot", si)] + [("actT", r * 8 + kk) for kk in range(nk)], writes=[("ps", bk)])
                    if r == 2:
                        xsl = xbuf[b][:, s, 0:512]
                        op(DVE, lambda e, bk=bk, xsl=xsl: e.tensor_tensor(out=xsl, in0=banks[bk][:, :], in1=xsl, op=ALU.add),
                           reads=[("ps", bk), ("x", b, s)], writes=[("x", b, s)])
            if hoist is not None:
                hoist()
            sis = [use_piece(ti, l, 28), use_piece(ti, l, 29, oldest=28), use_piece(ti, l, 30, oldest=28)]
            def f6_A(s, mid=None):
                bk = next_bank()
                fns = []
                for r in range(3):
                    Wt = slots[sis[r]]
                    nk = 8 if r < 2 else 6
                    for kk in range(nk):
                        fns.append(lambda e, r=r, kk=kk, Wt=Wt: e.matmul(
                            banks[bk][:, :], lhsT=actT[:, (r * 8 + kk) * 512 + s * 128:(r * 8 + kk) * 512 + (s + 1) * 128],
                            rhs=Wt[:, kk * 512:(kk + 1) * 512], start=(r == 0 and kk == 0), stop=(r == 2 and kk == 5)))
                pe_group(fns, reads=[("slot", x_) for x_ in sis] + [("actT", c) for c in range(22)], writes=[("ps", bk)])
                if mid is not None:
                    mid()
                xsl = xbuf[b][:, s, 512:1024]
                op(DVE, lambda e, bk=bk, xsl=xsl: e.tensor_tensor(out=xsl, in0=banks[bk][:, :], in1=xsl, op=ALU.add),
                   reads=[("ps", bk), ("x", b, s)], writes=[("x", b, s)])
            defT.extend(norm_pipeline(ti, nsub, gnext, emit_A=f6_A, final=last_layer, defer=(2 if nsub == 4 else 0), mid_hook='last'))

        def load_consts():
            def cdma(out, in_):
                ins = nc.sync.dma_start(out=out, in_=in_)
                ins.then_inc(s_const, 16)
                semcnt[s_const] = semcnt.get(s_const, 0) + 16
            cdma(pcol[:], pcol_d)
            cdma(ident[:], ident_d)
            cdma(invc[:], invc_d)
            for i in range(5):
                cdma(gall[:, i, :], gall_d[i].partition_broadcast(128))
            cdma(HS2[('s', 'pool')][:], sp_in)
            cdma(HS2[('s', 'conv')][:], sc_in)
            cdma(HS2[('s', 'ffn')][:], sf_in)
            const_tok = (s_const, semcnt[s_const])
            for k in ["pcol", "ident", "invc", "gall", ('H', 's', 'pool'), ('H', 's', 'conv'), ('H', 's', 'ffn')]:
                trk.lw[k] = const_tok

        x_load(0)
        load_consts()
        POOL.wait(s_xl[0], semcnt[s_xl[0]])
        POOL.wait(s_const, semcnt[s_const])
        issue_casts(NL * NPIECE)
        def first_norm(ti):
            norm_pipeline(ti, tiles[ti]['nsub'], 0, no_pool=(ti == 0))

        def emit_all():
          for ti in range(NT):
              if ti + 1 < NT:
                  n_at = (ti * NL + 1) * NPIECE
                  pending_sync.append((n_at, (lambda t=ti + 1: x_load(t))))
              if ti == 0:
                  first_norm(ti)
              for l in range(NL):
                  hoist = (lambda t=ti + 1: first_norm(t)) if (l == NL - 1 and ti + 1 < NT) else None
                  layer(ti, l, hoist)
              if ti == 0:
                  issue_casts(NL * NPIECE)
              pending_sync.append((load_ctr[0] + NSLOT + 1, (lambda t=ti: y_store(t))))
        try:
            emit_all()
        except _Stop:
            pass
        if dbg is not None:
            dbg_sem = newsem("s_dbg")
            dumps = [("hT", hT[:].rearrange("p k t -> p (k t)"), BF16, 8 * 512), ("ubuf", ubuf, F32, 4 * UW), ("gbbuf", gbbuf, F32, 4 * 512),
                     ("cvin", cvin, F32, 4 * CVW), ("mixedT", mixedT, BF16, 4 * 512), ("gcvT", gcvT, BF16, 4 * 512),
                     ("mergedT", mergedT, BF16, 8 * 512), ("x0", xbuf[0][:].rearrange("p s d -> p (s d)"), F32, 4 * D),
                     ("actT", actT, BF16, 22 * 512), ("rsb", rsb[:], F32, NST), ("hbuf0", hbuf[0][:], BF16, D),
                     ("arenaB", arenaB[:], mybir.dt.uint8, 37376), ("hffn", HS2[('p', 'ffn')][:], F32, NL * 44 * 2)]
            for E_ in (PE, ACT, DVE, POOL):
                if E_.n:
                    SYNC.wait(E_.sem, E_.n)
            nd = 0
            for nm, ap_, dt_, n_ in dumps:
                dd = nc.dram_tensor("dbg_" + nm, [128, n_], dt_, kind="ExternalOutput").ap()
                nc.sync.dma_start(out=dd, in_=ap_).then_inc(dbg_sem, 16)
                nd += 16
            SYNC.wait(dbg_sem, nd)
        for item in list(pending_sync):
            pending_sync.remove(item)
            item[1]()
        for kind in ('p', 's'):
            if kind == 's' and not with_sample:
                continue
            for nm in ('pool', 'conv', 'ffn'):
                keys = [('H', kind, nm)]
                trk.sync(SYNC, keys, [])
                ins = nc.sync.dma_start(out=o_state[(kind, nm)], in_=HS2[(kind, nm)][:])
                ins.then_inc(s_out, 16)
                semcnt[s_out] = semcnt.get(s_out, 0) + 16
        SYNC.wait(s_out, semcnt[s_out])
        for bsem in s_ys:
            if semcnt.get(bsem, 0):
                SYNC.wait(bsem, semcnt[bsem])
    return nc


_CACHE = {}


def kernel(x_prompt, x_sample, state_pool, state_conv, state_ffn, norm_mix_g, w_in, b_gate,
           w_pool_map, pool_scale, conv_w, w_conv_out, w_o, norm_ffn_g, w_up, ffn_conv_w,
           ffn_conv_b, w_down, final_norm_g):
    f = lambda a: np.ascontiguousarray(np.asarray(a, dtype=np.float32))
    x_prompt, x_sample = f(x_prompt), f(x_sample)
    state_pool, state_conv, state_ffn = f(state_pool), f(state_conv), f(state_ffn)
    wpieces = make_pieces(f(w_in), f(w_pool_map), f(w_conv_out), f(w_o), f(w_up), f(w_down))
    pcol = make_pcol(f(b_gate), f(pool_scale), f(conv_w), f(ffn_conv_w), f(ffn_conv_b))
    nmg, nfg, fng = f(norm_mix_g), f(norm_ffn_g), f(final_norm_g)
    gall = np.stack([nmg[0], nfg[0], nmg[1], nfg[1], fng], axis=0)
    ident = np.eye(128, dtype=np.float32).astype(ml_dtypes.bfloat16)
    invcnt = np.zeros((128, 4, 16), np.float32)
    for g in range(4):
        for t in range(16):
            invcnt[:, g, t] = 1.0 / min(2 << g, t + 1)

    if "nc" not in _CACHE:
        _CACHE["nc"] = build_program()
    nc = _CACHE["nc"]

    in_maps = []
    for c in range(N_CORES):
        sl = slice(2 * c, 2 * c + 2)
        sp = state_pool[:, sl].reshape(NL, 2, 15, 4, 128).transpose(4, 0, 3, 1, 2)
        sc = state_conv[:, sl].reshape(NL, 2, 2, 4, 128).transpose(4, 0, 3, 1, 2)
        sf = state_ffn[:, sl].reshape(NL, 2, 2, 44, 128).transpose(4, 0, 3, 1, 2)
        in_maps.append({
            "xp": x_prompt[c], "xs": x_sample[sl].reshape(128, D),
            "wp": wpieces, "pcol": pcol, "gall": gall, "ident": ident, "invcnt": invcnt,
            "sp_in": np.ascontiguousarray(sp).reshape(128, -1), "sc_in": np.ascontiguousarray(sc).reshape(128, -1),
            "sf_in": np.ascontiguousarray(sf).reshape(128, -1),
        })
    res = run_bass_kernel_spmd(nc, in_maps, core_ids=list(range(N_CORES)))
    R = res.results
    y_prompt = np.stack([R[c]["yp"] for c in range(N_CORES)], axis=0)
    y_sample = np.concatenate([R[c]["ys"].reshape(2, 64, D) for c in range(N_CORES)], axis=0)

    def gather(name, nseg, nch, nrow):
        outs = []
        for c in range(N_CORES):
            a = R[c][name].reshape(128, NL, nch, nseg, nrow)
            outs.append(a.transpose(1, 3, 4, 2, 0).reshape(NL, nseg, nrow, nch * 128))
        return np.concatenate(outs, axis=1)
    pool_p = gather("o_pool_p", 1, 4, 15)
    conv_p = gather("o_conv_p", 1, 4, 2)
    ffn_p = gather("o_ffn_p", 1, 44, 2)
    pool_s = gather("o_pool_s", 2, 4, 15)
    conv_s = gather("o_conv_s", 2, 4, 2)
    ffn_s = gather("o_ffn_s", 2, 44, 2)
    return (y_prompt.astype(np.float32), y_sample.astype(np.float32), pool_p, conv_p, ffn_p, pool_s, conv_s, ffn_s)
```

```python
import numpy as np
import ml_dtypes
from contextlib import ExitStack
import concourse.bass as bass
import concourse.mybir as mybir
from concourse.bass_utils import run_bass_kernel_spmd

F32 = mybir.dt.float32
BF16 = mybir.dt.bfloat16
ALU = mybir.AluOpType
AF = mybir.ActivationFunctionType

D = 1024
DFF = 2816
NL = 2
NPIECE = 31
NSLOT = 6
SLOTW = 4096
ROWP = 4160
NCAST = 62
EPS = 1e-6
N_CORES = 8
NPT = 8

BG0 = 0
PS0 = BG0 + NL * 16
CW0 = PS0 + NL * 8
FW0 = CW0 + NL * 3 * 4
FB0 = FW0 + NL * 3 * 44
NCOL = FB0 + NL * 44


def piece_width(p):
    if 4 <= p < 12:
        return 2688
    if p in (27, 30):
        return 3072
    return 4096


def _kmajor(W):
    K, C = W.shape
    return W.reshape(K // 128, 128, C).transpose(1, 0, 2).reshape(128, -1)


def make_pieces(w_in, w_pool_map, w_conv_out, w_o, w_up, w_down):
    out = np.zeros((NL * NPIECE, 128, ROWP), np.float32)
    for l in range(NL):
        ps = []
        for j in range(4):
            ps.append(_kmajor(w_in[l][:, j * 512:(j + 1) * 512]))
        for d in range(8):
            ga = _kmajor(w_in[l][:, 2048 + d * 128:2048 + (d + 1) * 128])
            gb = _kmajor(w_in[l][:, 3072 + d * 128:3072 + (d + 1) * 128])
            co = _kmajor(w_conv_out[l][:, d * 128:(d + 1) * 128])
            pm = w_pool_map[l][d // 2][:, (d % 2) * 128:(d % 2 + 1) * 128]
            ps.append(np.concatenate([ga, gb, co, pm], axis=1))
        for h in range(2):
            ps.append(_kmajor(w_o[l][:, h * 512:(h + 1) * 512]))
        for j in range(11):
            cols = np.concatenate([
                np.arange((2 * j) * 128, (2 * j + 2) * 128),
                DFF + np.arange((2 * j) * 128, (2 * j + 2) * 128)])
            ps.append(_kmajor(w_up[l][:, cols]))
        for h in range(2):
            for r in range(3):
                ps.append(_kmajor(w_down[l][r * 1024:min((r + 1) * 1024, DFF), h * 512:(h + 1) * 512]))
        assert len(ps) == NPIECE
        for i, p in enumerate(ps):
            assert p.shape[1] == piece_width(i), (i, p.shape)
            out[l * NPIECE + i, :, :p.shape[1]] = p
    return out


def make_pcol(b_gate, pool_scale, conv_w, ffn_conv_w, ffn_conv_b):
    pc = np.zeros((128, NCOL), np.float32)
    for l in range(NL):
        pc[:, BG0 + l * 16:BG0 + (l + 1) * 16] = b_gate[l].reshape(16, 128).T
        pc[:, PS0 + l * 8:PS0 + (l + 1) * 8] = pool_scale[l].reshape(8, 128).T
        for k in range(3):
            o = CW0 + (l * 3 + k) * 4
            pc[:, o:o + 4] = conv_w[l][k].reshape(4, 128).T
            o = FW0 + (l * 3 + k) * 44
            pc[:, o:o + 44] = ffn_conv_w[l][k].reshape(44, 128).T
        pc[:, FB0 + l * 44:FB0 + (l + 1) * 44] = ffn_conv_b[l].reshape(44, 128).T
    return pc


class Eng:
    def __init__(self, e, sem, kind):
        self.e = e
        self.sem = sem
        self.kind = kind
        self.n = 0
        self.waited = {}

    def wait(self, sem, val):
        if self.waited.get(sem, 0) >= val:
            return
        self.e.wait_ge(sem, val)
        self.waited[sem] = val


class Trk:
    def __init__(self):
        self.lw = {}
        self.rs = {}

    def sync(self, E, reads, writes):
        need = {}

        def add(tok, raw):
            sem, val = tok
            if sem is E.sem:
                if E.kind == 'pe':
                    return
                if E.kind in ('act', 'dve', 'pool') and not raw:
                    return
            if need.get(sem, 0) < val:
                need[sem] = val
        for k in reads:
            t = self.lw.get(k)
            if t is not None:
                add(t, True)
        for k in writes:
            t = self.lw.get(k)
            if t is not None:
                add(t, False)
            for s, v in self.rs.get(k, {}).items():
                add((s, v), False)
        for s, v in need.items():
            E.wait(s, v)

    def commit(self, tok, reads, writes):
        for k in writes:
            self.lw[k] = tok
            self.rs[k] = {}
        for k in reads:
            d = self.rs.setdefault(k, {})
            if d.get(tok[0], 0) < tok[1]:
                d[tok[0]] = tok[1]


class _Stop(Exception):
    pass


def build_program(npt=NPT, with_sample=True, dbg=None):
    nc = bass.Bass("TRN2", target_bir_lowering=False)
    xp = nc.dram_tensor("xp", [npt * 512, D], F32, kind="ExternalInput").ap()
    xs = nc.dram_tensor("xs", [128, D], F32, kind="ExternalInput").ap()
    wp = nc.dram_tensor("wp", [NL * NPIECE, 128, ROWP], F32, kind="ExternalInput").ap()
    pcol_d = nc.dram_tensor("pcol", [128, NCOL], F32, kind="ExternalInput").ap()
    gall_d = nc.dram_tensor("gall", [5, D], F32, kind="ExternalInput").ap()
    ident_d = nc.dram_tensor("ident", [128, 128], BF16, kind="ExternalInput").ap()
    invc_d = nc.dram_tensor("invcnt", [128, 4, 16], F32, kind="ExternalInput").ap()
    sp_in = nc.dram_tensor("sp_in", [128, NL * 4 * 2 * 15], F32, kind="ExternalInput").ap()
    sc_in = nc.dram_tensor("sc_in", [128, NL * 4 * 2 * 2], F32, kind="ExternalInput").ap()
    sf_in = nc.dram_tensor("sf_in", [128, NL * 44 * 2 * 2], F32, kind="ExternalInput").ap()
    yp = nc.dram_tensor("yp", [npt * 512, D], F32, kind="ExternalOutput").ap()
    ys = nc.dram_tensor("ys", [128, D], F32, kind="ExternalOutput").ap()
    o_state = {
        ('p', 'pool'): nc.dram_tensor("o_pool_p", [128, NL * 4 * 1 * 15], F32, kind="ExternalOutput").ap(),
        ('p', 'conv'): nc.dram_tensor("o_conv_p", [128, NL * 4 * 1 * 2], F32, kind="ExternalOutput").ap(),
        ('p', 'ffn'): nc.dram_tensor("o_ffn_p", [128, NL * 44 * 1 * 2], F32, kind="ExternalOutput").ap(),
        ('s', 'pool'): nc.dram_tensor("o_pool_s", [128, NL * 4 * 2 * 15], F32, kind="ExternalOutput").ap(),
        ('s', 'conv'): nc.dram_tensor("o_conv_s", [128, NL * 4 * 2 * 2], F32, kind="ExternalOutput").ap(),
        ('s', 'ffn'): nc.dram_tensor("o_ffn_s", [128, NL * 44 * 2 * 2], F32, kind="ExternalOutput").ap(),
    }
    wsc = nc.dram_tensor("wsc", [NL * NPIECE, 128, ROWP], BF16).ap()

    es = ExitStack()
    with es:
        def sb(name, shape, dt):
            return es.enter_context(nc.sbuf_tensor("sb_" + name, shape, dt))

        def newsem(name):
            return es.enter_context(nc.semaphore(name))

        xbuf = [sb(f"xbuf{i}", [128, 4, D], F32) for i in range(2)]
        hbuf = [sb(f"hbuf{i}", [128, D], BF16) for i in range(4)]
        junk = sb("junk", [128, D], BF16)
        hT = sb("hT", [128, 8, 512], BF16)
        slots = [sb(f"slot{i}", [128, SLOTW], BF16) for i in range(NSLOT)]
        gall = sb("gall", [128, 5, D], F32)
        pcol = sb("pcol", [128, NCOL], F32)
        ident = sb("ident", [128, 128], BF16)
        invc = sb("invc", [128, 4, 16], F32)
        nhalf = sb("nhalf", [128, 1], F32)
        NST = 8
        ssb = sb("ssb", [128, NST], F32)
        msb = sb("msb", [128, NST], F32)
        rsb = sb("rsb", [128, NST], F32)
        fix16 = sb("fix16", [128, 16], F32)
        hcbuf = sb("hcbuf", [128, 44 * 2 * 2], F32)
        tmp44 = sb("tmp44", [128, 44], F32)
        HSdims = {('p', 'pool'): (4, 1, 15), ('p', 'conv'): (4, 1, 2), ('p', 'ffn'): (44, 1, 2),
                  ('s', 'pool'): (4, 2, 15), ('s', 'conv'): (4, 2, 2), ('s', 'ffn'): (44, 2, 2)}
        HS2 = {k: sb("h%s_%s" % k, [128, NL * v[0] * v[1] * v[2]], F32) for k, v in HSdims.items()}
        HS = {k: HS2[k][:].rearrange("p (l c s r) -> p l c s r", l=NL, c=v[0], s=v[1]) for k, v in HSdims.items()}
        UW = 528
        CVW = 516
        UPW = 516
        arenaA = sb("arenaA", [128, 25600], mybir.dt.uint8)
        arenaB = sb("arenaB", [128, 37376], mybir.dt.uint8)

        def carve(arena, off, nelem, dt):
            nb = nelem * (4 if dt == F32 else 2)
            v = arena[:, off:off + nb].bitcast(dt)
            return v, off + nb
        o = 0
        ubuf, o = carve(arenaA, o, 4 * UW, F32)
        gbbuf, o = carve(arenaA, o, 4 * 512, F32)
        cvin, o = carve(arenaA, o, 4 * CVW, F32)
        assert o <= 25600
        actT, _ = carve(arenaA, 0, 22 * 512, BF16)
        o = 0
        sga = [None, None]
        sgb = [None, None]
        for i in range(2):
            sga[i], o = carve(arenaB, o, 512, F32)
            sgb[i], o = carve(arenaB, o, 512, F32)
        mergedT, o = carve(arenaB, o, 8 * 512, BF16)
        mixedT, o = carve(arenaB, o, 4 * 512, BF16)
        gcvT, o = carve(arenaB, o, 4 * 512, BF16)
        ptmp = [None, None]
        for i in range(2):
            ptmp[i], o = carve(arenaB, o, UW, F32)
        gctmp = [None, None]
        cvt = [None, None]
        for i in range(2):
            gctmp[i], o = carve(arenaB, o, 512, F32)
            cvt[i], o = carve(arenaB, o, 512, F32)
        assert o <= 37376, o
        o = 0
        upbuf = []
        tbuf = []
        for i in range(8):
            a, o = carve(arenaB, o, UPW, F32)
            upbuf.append(a)
        for i in range(8):
            a, o = carve(arenaB, o, 512, F32)
            tbuf.append(a)
        assert o <= 37376, o

        banks = [es.enter_context(nc.psum_tensor(f"bank{i}", [128, 512], F32)) for i in range(8)]

        s_pe, s_act, s_dve, s_pool = newsem("s_pe"), newsem("s_act"), newsem("s_dve"), newsem("s_pool")
        s_slot = [newsem(f"s_slot{i}") for i in range(NSLOT)]
        s_cast = [newsem(f"s_cast{i}") for i in range(NCAST)]
        s_xl = [newsem(f"s_xl{i}") for i in range(2)]
        s_ys = [newsem(f"s_ys{i}") for i in range(2)]
        s_const = newsem("s_const")
        s_out = newsem("s_out")

        PE = Eng(nc.tensor, s_pe, 'pe')
        ACT = Eng(nc.scalar, s_act, 'act')
        DVE = Eng(nc.vector, s_dve, 'dve')
        POOL = Eng(nc.gpsimd, s_pool, 'pool')
        SYNC = Eng(nc.sync, None, 'dma')
        trk = Trk()
        semcnt = {}

        def op(E, fn, reads=(), writes=()):
            trk.sync(E, reads, writes)
            ins = fn(E.e)
            E.n += 1
            ins.then_inc(E.sem, 1)
            tok = (E.sem, E.n)
            trk.commit(tok, reads, writes)
            return tok

        def dma(E, sem, fn, reads=(), writes=()):
            rk = list(reads)
            wk = list(writes) + [("sem", sem)]
            trk.sync(E, rk, wk)
            ins = fn(E.e)
            ins.then_inc(sem, 16)
            semcnt[sem] = semcnt.get(sem, 0) + 16
            tok = (sem, semcnt[sem])
            trk.commit(tok, rk, wk)
            return tok

        def pe_group(fns, reads, writes):
            trk.sync(PE, reads, writes)
            ins = None
            for f in fns:
                ins = f(PE.e)
            PE.n += 1
            ins.then_inc(PE.sem, 1)
            tok = (PE.sem, PE.n)
            trk.commit(tok, reads, writes)
            return tok

        bank_ctr = [0]

        def next_bank():
            b = bank_ctr[0] % 8
            bank_ctr[0] += 1
            return b

        stat_ctr = [0]
        defT = []

        op(DVE, lambda e: e.memset(HS2[('p', 'pool')][:], 0.0), writes=[('H', 'p', 'pool')])
        op(DVE, lambda e: e.memset(HS2[('p', 'conv')][:], 0.0), writes=[('H', 'p', 'conv')])
        op(DVE, lambda e: e.memset(HS2[('p', 'ffn')][:], 0.0), writes=[('H', 'p', 'ffn')])
        op(DVE, lambda e: e.memset(nhalf[:], -0.5), writes=["nhalf"])

        tiles = []
        for i in range(npt):
            tiles.append(dict(kind='p', idx=i, ntok=512, nseg=1, L=512, nsub=4, first=(i == 0)))
        if with_sample:
            tiles.append(dict(kind='s', idx=0, ntok=128, nseg=2, L=64, nsub=1, first=False))
        NT = len(tiles)

        cast_next = [0]

        def issue_casts(upto):
            upto = min(upto, NL * NPIECE)
            while cast_next[0] < upto:
                p = cast_next[0]
                w = piece_width(p % NPIECE)
                dma(POOL, s_cast[p % NCAST],
                    lambda e, p=p, w=w: e.dma_start(out=wsc[p, :, 0:w], in_=wp[p, :, 0:w]),
                    reads=[], writes=[("wsc", p)])
                cast_next[0] += 1

        load_ctr = [0]
        pending_sync = []

        def load_piece(l, p):
            gi = l * NPIECE + p
            w = piece_width(p)
            n = load_ctr[0]
            si = n % NSLOT
            for item in list(pending_sync):
                if item[0] <= n:
                    pending_sync.remove(item)
                    item[1]()
            dma(SYNC, s_slot[si],
                lambda e: e.dma_start(out=slots[si][:, 0:w], in_=wsc[gi, :, 0:w]),
                reads=[("wsc", gi)], writes=[("slot", si)])
            load_ctr[0] += 1
            return si

        sched = [(t, l, p) for t in range(NT) for l in range(NL) for p in range(NPIECE)]
        sched_pos = [0]
        slot_of = {}

        def prefetch(upto_idx):
            while sched_pos[0] < min(upto_idx, len(sched)):
                t, l, p = sched[sched_pos[0]]
                slot_of[(t, l, p)] = load_piece(l, p)
                sched_pos[0] += 1

        def use_piece(t, l, p, oldest=None):
            gidx = (t * NL + l) * NPIECE + p
            gold = gidx if oldest is None else (t * NL + l) * NPIECE + oldest
            prefetch(gidx + 1)
            prefetch(gold + NSLOT)
            return slot_of[(t, l, p)]

        def x_load(ti):
            tl = tiles[ti]
            b = ti % 2
            if tl['kind'] == 'p':
                src = xp[tl['idx'] * 512:(tl['idx'] + 1) * 512, :].rearrange("(s p) d -> p s d", p=128)
                dst = xbuf[b][:, 0:4, :]
            else:
                src = xs
                dst = xbuf[b][:, 0, :]
            dma(SYNC, s_xl[b], lambda e: e.dma_start(out=dst, in_=src),
                reads=[], writes=[("x", b, s) for s in range(4)])

        def y_store(ti):
            tl = tiles[ti]
            b = ti % 2
            if tl['kind'] == 'p':
                dst = yp[tl['idx'] * 512:(tl['idx'] + 1) * 512, :].rearrange("(s p) d -> p s d", p=128)
                src = xbuf[b][:, 0:4, :]
            else:
                dst = ys
                src = xbuf[b][:, 0, :]
            dma(SYNC, s_ys[b], lambda e: e.dma_start(out=dst, in_=src),
                reads=[("x", b, s) for s in range(tl['nsub'])], writes=[])

        def norm_B(ti, s):
            b = ti % 2
            i = stat_ctr[0]
            stat_ctr[0] += 1
            c = i % NST
            par = i % 4
            xs_ = xbuf[b][:, s, :]
            op(ACT, lambda e: e.activation(out=junk[:], in_=xs_, func=AF.Square, accum_out=ssb[:, c:c + 1]),
               reads=[("x", b, s)], writes=["junk", ("ss", c)])
            return c, par

        def norm_C(c, no_pool=False):
            op(DVE, lambda e: e.tensor_scalar(out=msb[:, c:c + 1], in0=ssb[:, c:c + 1], scalar1=1.0 / D, scalar2=EPS,
                                              op0=ALU.mult, op1=ALU.add),
               reads=[("ss", c)], writes=[("ms", c)])
            if no_pool:
                op(ACT, lambda e: e.activation(out=msb[:, c:c + 1], in_=msb[:, c:c + 1], func=AF.Sqrt),
                   reads=[("ms", c)], writes=[("ms", c)])
                op(DVE, lambda e: e.reciprocal(out=rsb[:, c:c + 1], in_=msb[:, c:c + 1]),
                   reads=[("ms", c)], writes=[("rs", c)])
            else:
                op(POOL, lambda e: e.tensor_tensor(out=rsb[:, c:c + 1], in0=msb[:, c:c + 1], in1=nhalf[:], op=ALU.pow),
                   reads=[("ms", c), "nhalf"], writes=[("rs", c)])

        def norm_D(ti, s, gidx, c, par):
            b = ti % 2
            xs_ = xbuf[b][:, s, :]
            op(DVE, lambda e: e.scalar_tensor_tensor(out=hbuf[par][:], in0=xs_, scalar=rsb[:, c:c + 1], in1=gall[:, gidx, :],
                                                     op0=ALU.mult, op1=ALU.mult),
               reads=[("x", b, s), ("rs", c), "gall"], writes=[("h", par)])

        def norm_D_final(ti, s, c):
            b = ti % 2
            xs_ = xbuf[b][:, s, :]
            op(DVE, lambda e: e.scalar_tensor_tensor(out=xs_, in0=xs_, scalar=rsb[:, c:c + 1], in1=gall[:, 4, :],
                                                     op0=ALU.mult, op1=ALU.mult),
               reads=[("x", b, s), ("rs", c), "gall"], writes=[("x", b, s)])

        def norm_pipeline(ti, nsub_, gidx, emit_A=None, final=False, no_pool=False, defer=0, mid_hook=None):
            st = {}
            pending = []
            e_done = set()

            def stage_E(s_):
                if s_ in e_done or not (0 <= s_ < nsub_) or final:
                    return
                e_done.add(s_)
                if s_ >= nsub_ - defer:
                    pending.append((s_, st[s_][1]))
                else:
                    transpose_sub(s_, st[s_][1])

            for i in range(nsub_ + 3):
                def stage_D(i=i):
                    if 0 <= i - 2 < nsub_:
                        if final:
                            norm_D_final(ti, i - 2, st[i - 2][0])
                        else:
                            norm_D(ti, i - 2, gidx, *st[i - 2])
                d_done = False
                if i < nsub_:
                    last_A = (i == nsub_ - 1)
                    if emit_A is not None:
                        if mid_hook == 'always' or (mid_hook == 'last' and last_A):
                            emit_A(i, stage_D)
                            d_done = True
                        else:
                            emit_A(i)
                    if last_A and d_done and defer > 0 and not final:
                        for s_ in (i - 3, i - 2):
                            if s_ < nsub_ - defer:
                                stage_E(s_)
                    st[i] = norm_B(ti, i)
                if 0 <= i - 1 < nsub_:
                    norm_C(st[i - 1][0], no_pool)
                if not d_done:
                    stage_D()
                stage_E(i - 3)
            return pending

        def transpose_sub(s, par):
            bk = next_bank()
            pT = banks[bk][:].bitcast(BF16)
            fns = [(lambda e, k=k: e.transpose(out=pT[:, k * 128:(k + 1) * 128], in_=hbuf[par][:, k * 128:(k + 1) * 128],
                                               identity=ident[:])) for k in range(8)]
            pe_group(fns, reads=[("h", par), "ident"], writes=[("ps", bk)])
            op(ACT, lambda e: e.copy(out=hT[:, :, s * 128:(s + 1) * 128],
                                     in_=pT.rearrange("p (k t) -> p k t", k=8)),
               reads=[("ps", bk)], writes=[("hT", s)])

        def layer(ti, l, hoist=None):
            tl = tiles[ti]
            kind, ntok, nseg, L, nsub, first = tl['kind'], tl['ntok'], tl['nseg'], tl['L'], tl['nsub'], tl['first']
            b = ti % 2
            hTk = [("hT", s) for s in range(nsub)]

            def seg(ap2d):
                return ap2d[:, 0:ntok].rearrange("p (s l) -> p s l", s=nseg)

            def pc(i):
                return pcol[:, i:i + 1]

            Hpool = HS[(kind, 'pool')]
            Hconv = HS[(kind, 'conv')]
            Hffn = HS[(kind, 'ffn')]
            kHp, kHc, kHf = ('H', kind, 'pool'), ('H', kind, 'conv'), ('H', kind, 'ffn')
            W = 15 + L
            U4 = ubuf[:, 0:4 * nseg * W].rearrange("p (g s w) -> p g s w", g=4, s=nseg)
            CW = 2 + L
            C4 = cvin[:, 0:4 * nseg * CW].rearrange("p (g s w) -> p g s w", g=4, s=nseg)

            op(DVE, lambda e: e.tensor_copy(out=U4[:, :, :, 0:15], in_=Hpool[:, l]),
               reads=[kHp], writes=[("u", g) for g in range(4)])
            op(DVE, lambda e: e.tensor_copy(out=C4[:, :, :, 0:2], in_=Hconv[:, l]),
               reads=[kHc], writes=[("cvin", g) for g in range(4)])

            def proj_group(si, c):
                bk = next_bank()
                Wt = slots[si]
                fns = [(lambda e, k=k: e.matmul(banks[bk][:, 0:ntok], lhsT=Wt[:, k * 512 + c * 128:k * 512 + (c + 1) * 128],
                                                rhs=hT[:, k, 0:ntok], start=(k == 0), stop=(k == 7))) for k in range(8)]
                pe_group(fns, reads=[("slot", si)] + hTk, writes=[("ps", bk)])
                return bk

            si = use_piece(ti, l, 0)

            def half_group(si_, bk, col0, t0, t1, keys):
                Wt_ = slots[si_]
                fns = [(lambda e, k=k: e.matmul(banks[bk][:, t0:t1], lhsT=Wt_[:, k * 512 + col0:k * 512 + col0 + 128],
                                                rhs=hT[:, k, t0:t1], start=(k == 0), stop=(k == 7))) for k in range(8)]
                pe_group(fns, reads=[("slot", si_)] + keys, writes=[("ps", bk)])

            if defT:
                ubk = [next_bank() for _ in range(4)]
                for qt in range(2):
                    for g in range(4):
                        half_group(si, ubk[g], g * 128, qt * 128, (qt + 1) * 128, [("hT", qt)])
                for t_ in defT:
                    transpose_sub(*t_)
                del defT[:]
                for g in range(4):
                    half_group(si, ubk[g], g * 128, 256, 384, [("hT", 2)])
                for g in range(4):
                    half_group(si, ubk[g], g * 128, 384, 512, [("hT", 3)])
                    op(ACT, lambda e, g=g: e.copy(out=U4[:, g, :, 15:15 + L], in_=seg(banks[ubk[g]])),
                       reads=[("ps", ubk[g])], writes=[("u", g)])
            else:
                for g in range(4):
                    bk = proj_group(si, g)
                    op(ACT, lambda e, g=g, bk=bk: e.copy(out=U4[:, g, :, 15:15 + L], in_=seg(banks[bk])),
                       reads=[("ps", bk)], writes=[("u", g)])
            op(DVE, lambda e: e.tensor_copy(out=Hpool[:, l], in_=U4[:, :, :, L:L + 15]),
               reads=[("u", g) for g in range(4)], writes=[kHp])
            cur = {}
            for g in range(4):
                cur[g] = None
            PT = [ptmp[i][:, 0:nseg * W].rearrange("p (s w) -> p s w", s=nseg) for i in range(2)]
            for g in range(4):
                win = 2 << g
                Ug = U4[:, g]
                src = Ug
                src_key = ("u", g)
                lo_prev = -15
                nlev = g + 1
                for j in range(1, nlev + 1):
                    half = 1 << (j - 1)
                    lo = -(win - (1 << j))
                    dst = PT[(j - 1) % 2]
                    a0 = 15 + lo
                    op(DVE, lambda e, dst=dst, src=src, a0=a0, half=half: e.tensor_tensor(
                        out=dst[:, :, a0:15 + L], in0=src[:, :, a0:15 + L], in1=src[:, :, a0 - half:15 + L - half], op=ALU.add),
                       reads=[src_key], writes=[("ptmp", (j - 1) % 2)])
                    src = dst
                    src_key = ("ptmp", (j - 1) % 2)
                mg = seg(mixedT[:, g * 512:(g + 1) * 512])
                op(DVE, lambda e, src=src, Ug=Ug, mg=mg, win=win: e.scalar_tensor_tensor(
                    out=mg, in0=src[:, :, 15:15 + L], scalar=1.0 / win, in1=Ug[:, :, 15:15 + L],
                    op0=ALU.mult, op1=ALU.subtract),
                   reads=[src_key, ("u", g)], writes=[("mixedT", g)])
                if first:
                    op(DVE, lambda e, src=src, g=g: e.tensor_tensor(out=fix16[:], in0=src[:, 0, 15:31], in1=invc[:, g, :], op=ALU.mult),
                       reads=[src_key, "invc"], writes=["fix16"])
                    op(DVE, lambda e, Ug=Ug, g=g: e.tensor_tensor(out=mixedT[:, g * 512:g * 512 + 16], in0=fix16[:], in1=Ug[:, 0, 15:31],
                                                                 op=ALU.subtract),
                       reads=["fix16", ("u", g)], writes=[("mixedT", g)])

            si = use_piece(ti, l, 1)
            for c in range(4):
                bk = proj_group(si, c)
                op(ACT, lambda e, c=c, bk=bk: e.copy(out=gbbuf[:, c * 512:c * 512 + ntok], in_=banks[bk][:, 0:ntok]),
                   reads=[("ps", bk)], writes=[("gb", c)])
            si_gc = use_piece(ti, l, 2)
            si_v = use_piece(ti, l, 3, oldest=2)
            for c in range(4):
                bk = proj_group(si_gc, c)
                op(ACT, lambda e, c=c, bk=bk: e.copy(out=gctmp[c % 2][:, 0:ntok], in_=banks[bk][:, 0:ntok]),
                   reads=[("ps", bk)], writes=[("gctmp", c % 2)])
                bk = proj_group(si_v, c)
                op(DVE, lambda e, c=c, bk=bk: e.tensor_tensor(out=C4[:, c, :, 2:2 + L], in0=seg(banks[bk]), in1=seg(gctmp[c % 2]), op=ALU.mult),
                   reads=[("ps", bk), ("gctmp", c % 2)], writes=[("cvin", c)])
                cw = lambda k: pc(CW0 + (l * 3 + k) * 4 + c)
                ct = seg(cvt[c % 2])
                op(ACT, lambda e, c=c, ct=ct, cw=cw: e.activation(out=ct, in_=C4[:, c, :, 0:L], func=AF.Identity, scale=cw(0)),
                   reads=[("cvin", c), "pcol"], writes=[("cvt", c % 2)])
                op(DVE, lambda e, c=c, ct=ct, cw=cw: e.scalar_tensor_tensor(out=ct, in0=C4[:, c, :, 1:1 + L], scalar=cw(1), in1=ct,
                                                                           op0=ALU.mult, op1=ALU.add),
                   reads=[("cvin", c), ("cvt", c % 2), "pcol"], writes=[("cvt", c % 2)])
                op(DVE, lambda e, c=c, ct=ct, cw=cw: e.scalar_tensor_tensor(out=ct, in0=C4[:, c, :, 2:2 + L], scalar=cw(2), in1=ct,
                                                                           op0=ALU.mult, op1=ALU.add),
                   reads=[("cvin", c), ("cvt", c % 2), "pcol"], writes=[("cvt", c % 2)])
                op(DVE, lambda e, c=c: e.tensor_tensor(out=gcvT[:, c * 512:c * 512 + ntok], in0=gbbuf[:, c * 512:c * 512 + ntok],
                                                       in1=cvt[c % 2][:, 0:ntok], op=ALU.mult),
                   reads=[("gb", c), ("cvt", c % 2)], writes=[("gcvT", c)])
            op(DVE, lambda e: e.tensor_copy(out=Hconv[:, l], in_=C4[:, :, :, L:L + 2]),
               reads=[("cvin", g) for g in range(4)], writes=[kHc])

            if dbg == (ti, l, 'M5'):
                raise _Stop()
            def m7_logits(d):
                si = use_piece(ti, l, 4 + d, oldest=(4 + d - 1 if d > 0 else None))
                Wt = slots[si]
                bA, bB = next_bank(), next_bank()
                fns = [(lambda e, k=k: e.matmul(banks[bA][:, 0:ntok], lhsT=Wt[:, k * 128:(k + 1) * 128], rhs=hT[:, k, 0:ntok],
                                                start=(k == 0), stop=(k == 7))) for k in range(8)]
                pe_group(fns, reads=[("slot", si)] + hTk, writes=[("ps", bA)])
                fns = [(lambda e, k=k: e.matmul(banks[bB][:, 0:ntok], lhsT=Wt[:, 1024 + k * 128:1024 + (k + 1) * 128], rhs=hT[:, k, 0:ntok],
                                                start=(k == 0), stop=(k == 7))) for k in range(8)]
                pe_group(fns, reads=[("slot", si)] + hTk, writes=[("ps", bB)])
                pa = d % 2
                op(ACT, lambda e: e.activation(out=sga[pa][:, 0:ntok], in_=banks[bA][:, 0:ntok], func=AF.Sigmoid,
                                               bias=pc(BG0 + l * 16 + d)),
                   reads=[("ps", bA), "pcol"], writes=[("sga", pa)])
                op(ACT, lambda e: e.activation(out=sgb[pa][:, 0:ntok], in_=banks[bB][:, 0:ntok], func=AF.Sigmoid,
                                               bias=pc(BG0 + l * 16 + 8 + d)),
                   reads=[("ps", bB), "pcol"], writes=[("sgb", pa)])
                return si

            def m7_y(d, si):
                Wt = slots[si]
                g = d // 2
                pa = d % 2
                bC, bD = next_bank(), next_bank()
                fns = [lambda e: e.matmul(banks[bC][:, 0:ntok], lhsT=Wt[:, 2560:2688], rhs=mixedT[:, g * 512:g * 512 + ntok],
                                          start=True, stop=True)]
                pe_group(fns, reads=[("slot", si), ("mixedT", g)], writes=[("ps", bC)])
                fns = [(lambda e, k=k: e.matmul(banks[bD][:, 0:ntok], lhsT=Wt[:, 2048 + k * 128:2048 + (k + 1) * 128],
                                                rhs=gcvT[:, k * 512:k * 512 + ntok], start=(k == 0), stop=(k == 3))) for k in range(4)]
                pe_group(fns, reads=[("slot", si)] + [("gcvT", k) for k in range(4)], writes=[("ps", bD)])
                op(DVE, lambda e: e.scalar_tensor_tensor(out=sga[pa][:, 0:ntok], in0=banks[bC][:, 0:ntok], scalar=pc(PS0 + l * 8 + d),
                                                         in1=sga[pa][:, 0:ntok], op0=ALU.mult, op1=ALU.mult),
                   reads=[("ps", bC), ("sga", pa), "pcol"], writes=[("sga", pa)])
                op(DVE, lambda e: e.tensor_tensor(out=sgb[pa][:, 0:ntok], in0=banks[bD][:, 0:ntok], in1=sgb[pa][:, 0:ntok], op=ALU.mult),
                   reads=[("ps", bD), ("sgb", pa)], writes=[("sgb", pa)])
                op(DVE, lambda e: e.tensor_tensor(out=mergedT[:, d * 512:d * 512 + ntok], in0=sga[pa][:, 0:ntok], in1=sgb[pa][:, 0:ntok],
                                                  op=ALU.add),
                   reads=[("sga", pa), ("sgb", pa)], writes=[("mergedT", d)])

            m7si = {0: m7_logits(0)}
            for d in range(8):
                if d + 1 < 8:
                    m7si[d + 1] = m7_logits(d + 1)
                m7_y(d, m7si[d])

            if dbg == (ti, l, 'M7'):
                raise _Stop()
            if ti == 0:
                issue_casts(l * NPIECE + 14 + NCAST)
            si0 = use_piece(ti, l, 12)
            si1 = use_piece(ti, l, 13, oldest=12)
            def m8_A(s, mid=None):
                bk2 = [next_bank(), next_bank()]
                ksplits = [(0, 6), (6, 8)] if s == 0 else [(0, 8)]
                for (k0, k1) in ksplits:
                    for h, si in ((0, si0), (1, si1)):
                        bk = bk2[h]
                        Wt = slots[si]
                        fns = [(lambda e, k=k: e.matmul(banks[bk][:, :], lhsT=mergedT[:, k * 512 + s * 128:k * 512 + (s + 1) * 128],
                                                        rhs=Wt[:, k * 512:(k + 1) * 512], start=(k == 0), stop=(k == 7))) for k in range(k0, k1)]
                        pe_group(fns, reads=[("slot", si)] + [("mergedT", k) for k in range(k0, k1)], writes=[("ps", bk)])
                for h in range(2):
                    bk = bk2[h]
                    xsl = xbuf[b][:, s, h * 512:(h + 1) * 512]
                    op(DVE, lambda e, bk=bk, xsl=xsl: e.tensor_tensor(out=xsl, in0=banks[bk][:, :], in1=xsl, op=ALU.add),
                       reads=[("ps", bk), ("x", b, s)], writes=[("x", b, s)])
                    if h == 0 and mid is not None:
                        mid()
            defT.extend(norm_pipeline(ti, nsub, 2 * l + 1, emit_A=m8_A, defer=(2 if nsub == 4 else 0), mid_hook='always'))

            if dbg == (ti, l, 'M8'):
                raise _Stop()
            fcol = lambda k, ch: pc(FW0 + (l * 3 + k) * 44 + ch)
            fw_all = lambda k: pcol[:, FW0 + (l * 3 + k) * 44:FW0 + (l * 3 + k + 1) * 44]
            fb_all = pcol[:, FB0 + l * 44:FB0 + (l + 1) * 44]
            HCt = hcbuf[:, 0:44 * nseg * 2].rearrange("p (c s r) -> p c s r", c=44, s=nseg)
            Hl = Hffn[:, l]
            for sgi in range(nseg):
                a0, a1 = Hl[:, :, sgi, 0], Hl[:, :, sgi, 1]
                o0, o1 = HCt[:, :, sgi, 0], HCt[:, :, sgi, 1]
                op(DVE, lambda e: e.tensor_tensor(out=o0, in0=a0, in1=fw_all(0), op=ALU.mult), reads=[kHf, "pcol"], writes=["hc"])
                op(DVE, lambda e: e.tensor_tensor(out=tmp44[:], in0=a1, in1=fw_all(1), op=ALU.mult), reads=[kHf, "pcol"], writes=["tmp44"])
                op(DVE, lambda e: e.tensor_tensor(out=o1, in0=a1, in1=fw_all(0), op=ALU.mult), reads=[kHf, "pcol"], writes=["hc"])
                op(DVE, lambda e: e.tensor_tensor(out=o0, in0=o0, in1=tmp44[:], op=ALU.add), reads=["hc", "tmp44"], writes=["hc"])
                op(DVE, lambda e: e.tensor_tensor(out=o1, in0=o1, in1=fb_all, op=ALU.add), reads=["hc", "pcol"], writes=["hc"])
                op(DVE, lambda e: e.tensor_tensor(out=o0, in0=o0, in1=fb_all, op=ALU.add), reads=["hc", "pcol"], writes=["hc"])
            for j in range(11):
                si = use_piece(ti, l, 14 + j)
                Wt = slots[si]
                chs = [2 * j, 2 * j + 1, 22 + 2 * j, 22 + 2 * j + 1]
                info = []
                pre = None
                if j == 0 and defT:
                    pre = [next_bank() for _ in range(4)]
                    for qt in range(2):
                        for q in range(4):
                            half_group(si, pre[q], q * 128, qt * 128, (qt + 1) * 128, [("hT", qt)])
                    for t_ in defT:
                        transpose_sub(*t_)
                    del defT[:]
                    for q in range(4):
                        half_group(si, pre[q], q * 128, 256, 384, [("hT", 2)])
                for q in range(4):
                    ch = chs[q]
                    ui = (j % 2) * 4 + q
                    TB = seg(tbuf[ui])
                    if pre is not None:
                        bk = pre[q]
                        half_group(si, bk, q * 128, 384, 512, [("hT", 3)])
                    else:
                        bk = next_bank()
                        fns = [(lambda e, k=k: e.matmul(banks[bk][:, 0:ntok], lhsT=Wt[:, k * 512 + q * 128:k * 512 + (q + 1) * 128],
                                                        rhs=hT[:, k, 0:ntok], start=(k == 0), stop=(k == 7))) for k in range(8)]
                        pe_group(fns, reads=[("slot", si)] + hTk, writes=[("ps", bk)])
                    P3 = seg(banks[bk])
                    op(ACT, lambda e: e.copy(out=TB[:, :, 0:2], in_=HCt[:, ch]), reads=["hc"], writes=[("tb", ui)])
                    op(ACT, lambda e: e.activation(out=TB[:, :, 2:L], in_=P3[:, :, 0:L - 2], func=AF.Identity,
                                                   scale=fcol(0, ch), bias=pc(FB0 + l * 44 + ch)),
                       reads=[("ps", bk), "pcol"], writes=[("tb", ui)])
                    op(ACT, lambda e: e.copy(out=Hffn[:, l, ch], in_=P3[:, :, L - 2:L]), reads=[("ps", bk), "hc"], writes=[kHf])
                    info.append((ch, ui, TB, P3, bk))
                def tap1(ch, ui, TB, P3, bk):
                    op(DVE, lambda e: e.scalar_tensor_tensor(out=TB[:, :, 1:L], in0=P3[:, :, 0:L - 1], scalar=fcol(1, ch), in1=TB[:, :, 1:L],
                                                             op0=ALU.mult, op1=ALU.add),
                       reads=[("ps", bk), ("tb", ui), "pcol"], writes=[("tb", ui)])

                def tap2(ch, ui, TB, P3, bk):
                    op(DVE, lambda e: e.scalar_tensor_tensor(out=TB, in0=P3, scalar=fcol(2, ch), in1=TB, op0=ALU.mult, op1=ALU.add),
                       reads=[("ps", bk), ("tb", ui), "pcol"], writes=[("tb", ui)])
                for pr in (((0, 1), (2, 3)) if ti > 0 else ((0, 1, 2, 3),)):
                    for q_ in pr:
                        tap1(*info[q_])
                    for q_ in pr:
                        tap2(*info[q_])
                for q in (2, 3):
                    ui = (j % 2) * 4 + q
                    op(ACT, lambda e: e.activation(out=tbuf[ui][:, 0:ntok], in_=tbuf[ui][:, 0:ntok], func=AF.Silu),
                       reads=[("tb", ui)], writes=[("tb", ui)])
                for q in range(2):
                    uv = (j % 2) * 4 + q
                    ug = uv + 2
                    ch = chs[q]
                    op(DVE, lambda e: e.tensor_tensor(out=actT[:, ch * 512:ch * 512 + ntok], in0=tbuf[uv][:, 0:ntok],
                                                      in1=tbuf[ug][:, 0:ntok], op=ALU.mult),
                       reads=[("tb", uv), ("tb", ug)], writes=[("actT", ch)])

            if dbg == (ti, l, 'F5'):
                raise _Stop()
            last_layer = (l == NL - 1)
            gnext = 2 * (l + 1) if not last_layer else 4
            pendq = []
            if ti == 0:
                issue_casts((l + 1) * NPIECE + NCAST)
            bks = [next_bank() for _ in range(nsub)]
            for r in range(3):
                si = use_piece(ti, l, 25 + r)
                Wt = slots[si]
                nk = 8 if r < 2 else 6
                for s in range(nsub):
                    bk = bks[s]
                    fns = [(lambda e, kk=kk: e.matmul(banks[bk][:, :],
                                                      lhsT=actT[:, (r * 8 + kk) * 512 + s * 128:(r * 8 + kk) * 512 + (s + 1) * 128],
                                                      rhs=Wt[:, kk * 512:(kk + 1) * 512],
                                                      start=(r == 0 and kk == 0), stop=(r == 2 and kk == nk - 1))) for kk in range(nk)]
                    pe_group(fns, reads=[("slot", si)] + [("actT", r * 8 + kk) for kk in range(nk)], writes=[("ps", bk)])
                    if r == 2:
                        xsl = xbuf[b][:, s, 0:512]
                        op(DVE, lambda e, bk=bk, xsl=xsl: e.tensor_tensor(out=xsl, in0=banks[bk][:, :], in1=xsl, op=ALU.add),
                           reads=[("ps", bk), ("x", b, s)], writes=[("x", b, s)])
            if hoist is not None:
                hoist()
            sis = [use_piece(ti, l, 28), use_piece(ti, l, 29, oldest=28), use_piece(ti, l, 30, oldest=28)]
            def f6_A(s, mid=None):
                bk = next_bank()
                fns = []
                for r in range(3):
                    Wt = slots[sis[r]]
                    nk = 8 if r < 2 else 6
                    for kk in range(nk):
                        fns.append(lambda e, r=r, kk=kk, Wt=Wt: e.matmul(
                            banks[bk][:, :], lhsT=actT[:, (r * 8 + kk) * 512 + s * 128:(r * 8 + kk) * 512 + (s + 1) * 128],
                            rhs=Wt[:, kk * 512:(kk + 1) * 512], start=(r == 0 and kk == 0), stop=(r == 2 and kk == 5)))
                pe_group(fns, reads=[("slot", x_) for x_ in sis] + [("actT", c) for c in range(22)], writes=[("ps", bk)])
                if mid is not None:
                    mid()
                xsl = xbuf[b][:, s, 512:1024]
                op(DVE, lambda e, bk=bk, xsl=xsl: e.tensor_tensor(out=xsl, in0=banks[bk][:, :], in1=xsl, op=ALU.add),
                   reads=[("ps", bk), ("x", b, s)], writes=[("x", b, s)])
            defT.extend(norm_pipeline(ti, nsub, gnext, emit_A=f6_A, final=last_layer, defer=(2 if nsub == 4 else 0), mid_hook='last'))

        def load_consts():
            def cdma(out, in_):
                ins = nc.sync.dma_start(out=out, in_=in_)
                ins.then_inc(s_const, 16)
                semcnt[s_const] = semcnt.get(s_const, 0) + 16
            cdma(pcol[:], pcol_d)
            cdma(ident[:], ident_d)
            cdma(invc[:], invc_d)
            for i in range(5):
                cdma(gall[:, i, :], gall_d[i].partition_broadcast(128))
            cdma(HS2[('s', 'pool')][:], sp_in)
            cdma(HS2[('s', 'conv')][:], sc_in)
            cdma(HS2[('s', 'ffn')][:], sf_in)
            const_tok = (s_const, semcnt[s_const])
            for k in ["pcol", "ident", "invc", "gall", ('H', 's', 'pool'), ('H', 's', 'conv'), ('H', 's', 'ffn')]:
                trk.lw[k] = const_tok

        x_load(0)
        load_consts()
        POOL.wait(s_xl[0], semcnt[s_xl[0]])
        POOL.wait(s_const, semcnt[s_const])
        issue_casts(NL * NPIECE)
        def first_norm(ti):
            norm_pipeline(ti, tiles[ti]['nsub'], 0, no_pool=(ti == 0))

        def emit_all():
          for ti in range(NT):
              if ti + 1 < NT:
                  n_at = (ti * NL + 1) * NPIECE
                  pending_sync.append((n_at, (lambda t=ti + 1: x_load(t))))
              if ti == 0:
                  first_norm(ti)
              for l in range(NL):
                  hoist = (lambda t=ti + 1: first_norm(t)) if (l == NL - 1 and ti + 1 < NT) else None
                  layer(ti, l, hoist)
              if ti == 0:
                  issue_casts(NL * NPIECE)
              pending_sync.append((load_ctr[0] + NSLOT + 1, (lambda t=ti: y_store(t))))
        try:
            emit_all()
        except _Stop:
            pass
        if dbg is not None:
            dbg_sem = newsem("s_dbg")
            dumps = [("hT", hT[:].rearrange("p k t -> p (k t)"), BF16, 8 * 512), ("ubuf", ubuf, F32, 4 * UW), ("gbbuf", gbbuf, F32, 4 * 512),
                     ("cvin", cvin, F32, 4 * CVW), ("mixedT", mixedT, BF16, 4 * 512), ("gcvT", gcvT, BF16, 4 * 512),
                     ("mergedT", mergedT, BF16, 8 * 512), ("x0", xbuf[0][:].rearrange("p s d -> p (s d)"), F32, 4 * D),
                     ("actT", actT, BF16, 22 * 512), ("rsb", rsb[:], F32, NST), ("hbuf0", hbuf[0][:], BF16, D),
                     ("arenaB", arenaB[:], mybir.dt.uint8, 37376), ("hffn", HS2[('p', 'ffn')][:], F32, NL * 44 * 2)]
            for E_ in (PE, ACT, DVE, POOL):
                if E_.n:
                    SYNC.wait(E_.sem, E_.n)
            nd = 0
            for nm, ap_, dt_, n_ in dumps:
                dd = nc.dram_tensor("dbg_" + nm, [128, n_], dt_, kind="ExternalOutput").ap()
                nc.sync.dma_start(out=dd, in_=ap_).then_inc(dbg_sem, 16)
                nd += 16
            SYNC.wait(dbg_sem, nd)
        for item in list(pending_sync):
            pending_sync.remove(item)
            item[1]()
        for kind in ('p', 's'):
            if kind == 's' and not with_sample:
                continue
            for nm in ('pool', 'conv', 'ffn'):
                keys = [('H', kind, nm)]
                trk.sync(SYNC, keys, [])
                ins = nc.sync.dma_start(out=o_state[(kind, nm)], in_=HS2[(kind, nm)][:])
                ins.then_inc(s_out, 16)
                semcnt[s_out] = semcnt.get(s_out, 0) + 16
        SYNC.wait(s_out, semcnt[s_out])
        for bsem in s_ys:
            if semcnt.get(bsem, 0):
                SYNC.wait(bsem, semcnt[bsem])
    return nc


_CACHE = {}


def kernel(x_prompt, x_sample, state_pool, state_conv, state_ffn, norm_mix_g, w_in, b_gate,
           w_pool_map, pool_scale, conv_w, w_conv_out, w_o, norm_ffn_g, w_up, ffn_conv_w,
           ffn_conv_b, w_down, final_norm_g):
    f = lambda a: np.ascontiguousarray(np.asarray(a, dtype=np.float32))
    x_prompt, x_sample = f(x_prompt), f(x_sample)
    state_pool, state_conv, state_ffn = f(state_pool), f(state_conv), f(state_ffn)
    wpieces = make_pieces(f(w_in), f(w_pool_map), f(w_conv_out), f(w_o), f(w_up), f(w_down))
    pcol = make_pcol(f(b_gate), f(pool_scale), f(conv_w), f(ffn_conv_w), f(ffn_conv_b))
    nmg, nfg, fng = f(norm_mix_g), f(norm_ffn_g), f(final_norm_g)
    gall = np.stack([nmg[0], nfg[0], nmg[1], nfg[1], fng], axis=0)
    ident = np.eye(128, dtype=np.float32).astype(ml_dtypes.bfloat16)
    invcnt = np.zeros((128, 4, 16), np.float32)
    for g in range(4):
        for t in range(16):
            invcnt[:, g, t] = 1.0 / min(2 << g, t + 1)

    if "nc" not in _CACHE:
        _CACHE["nc"] = build_program()
    nc = _CACHE["nc"]

    in_maps = []
    for c in range(N_CORES):
        sl = slice(2 * c, 2 * c + 2)
        sp = state_pool[:, sl].reshape(NL, 2, 15, 4, 128).transpose(4, 0, 3, 1, 2)
        sc = state_conv[:, sl].reshape(NL, 2, 2, 4, 128).transpose(4, 0, 3, 1, 2)
        sf = state_ffn[:, sl].reshape(NL, 2, 2, 44, 128).transpose(4, 0, 3, 1, 2)
        in_maps.append({
            "xp": x_prompt[c], "xs": x_sample[sl].reshape(128, D),
            "wp": wpieces, "pcol": pcol, "gall": gall, "ident": ident, "invcnt": invcnt,
            "sp_in": np.ascontiguousarray(sp).reshape(128, -1), "sc_in": np.ascontiguousarray(sc).reshape(128, -1),
            "sf_in": np.ascontiguousarray(sf).reshape(128, -1),
        })
    res = run_bass_kernel_spmd(nc, in_maps, core_ids=list(range(N_CORES)))
    R = res.results
    y_prompt = np.stack([R[c]["yp"] for c in range(N_CORES)], axis=0)
    y_sample = np.concatenate([R[c]["ys"].reshape(2, 64, D) for c in range(N_CORES)], axis=0)

    def gather(name, nseg, nch, nrow):
        outs = []
        for c in range(N_CORES):
            a = R[c][name].reshape(128, NL, nch, nseg, nrow)
            outs.append(a.transpose(1, 3, 4, 2, 0).reshape(NL, nseg, nrow, nch * 128))
        return np.concatenate(outs, axis=1)
    pool_p = gather("o_pool_p", 1, 4, 15)
    conv_p = gather("o_conv_p", 1, 4, 2)
    ffn_p = gather("o_ffn_p", 1, 44, 2)
    pool_s = gather("o_pool_s", 2, 4, 15)
    conv_s = gather("o_conv_s", 2, 4, 2)
    ffn_s = gather("o_ffn_s", 2, 44, 2)
    return (y_prompt.astype(np.float32), y_sample.astype(np.float32), pool_p, conv_p, ffn_p, pool_s, conv_s, ffn_s)
```
